# Optimizing a Trainium2 kernel written in Bass

```python
import jax, jax.numpy as jnp
from jax import lax
import numpy as np

D_MODEL = 4096
BATCH = 4
SEQ = 4096
DEPTH = 1

HEAD_DIM = 128
ATTN_WIDTH = D_MODEL // 2
N_ATTN_HEADS = ATTN_WIDTH // HEAD_DIM
POOL_WIDTH = D_MODEL - ATTN_WIDTH
POOL_WINDOWS = (2, 4, 8, 16)
N_POOL_GROUPS = len(POOL_WINDOWS)
POOL_GROUP_WIDTH = POOL_WIDTH // N_POOL_GROUPS
IN_WIDTH = 3 * ATTN_WIDTH + POOL_WIDTH
MOBA_BLOCK = 256
MOBA_TOPK = 3
Q_CHUNK = 16
D_FF = 4 * D_MODEL
ROPE_THETA = 10000.0
LN_EPS = 1e-5
DEEPNORM_ALPHA = (2.0 * DEPTH) ** 0.25
DEEPNORM_BETA = (8.0 * DEPTH) ** -0.25

kernel_name = "hymba_moba_pool_deepnorm_layer"


def layer_norm(x, g, b):
    xf = x.astype(jnp.float32)
    mu = jnp.mean(xf, axis=-1, keepdims=True)
    var = jnp.mean(jnp.square(xf - mu), axis=-1, keepdims=True)
    y = (xf - mu) * lax.rsqrt(var + LN_EPS)
    return (y * g.astype(jnp.float32) + b.astype(jnp.float32)).astype(x.dtype)


def rope(t):
    S_, D_ = t.shape[2], t.shape[3]
    inv_freq = 1.0 / (ROPE_THETA ** (jnp.arange(0, D_, 2, dtype=jnp.float32) / D_))
    ang = jnp.arange(S_, dtype=jnp.float32)[:, None] * inv_freq[None, :]
    cos = jnp.cos(ang).astype(t.dtype)
    sin = jnp.sin(ang).astype(t.dtype)
    t1, t2 = t[..., : D_ // 2], t[..., D_ // 2:]
    return jnp.concatenate([t1 * cos - t2 * sin, t1 * sin + t2 * cos], axis=-1)


def moba_attention(q, k, v):
    B_, H_, S_, D_ = q.shape
    nb = -(-S_ // MOBA_BLOCK)
    pad = nb * MOBA_BLOCK - S_
    kp = jnp.pad(k, ((0, 0), (0, 0), (0, pad), (0, 0)))
    vp = jnp.pad(v, ((0, 0), (0, 0), (0, pad), (0, 0)))
    kb = kp.reshape(B_, H_, nb, MOBA_BLOCK, D_)
    vb = vp.reshape(B_, H_, nb, MOBA_BLOCK, D_)
    kmean = jnp.mean(kb.astype(jnp.float32), axis=3).astype(k.dtype)
    topk = min(MOBA_TOPK, nb)
    scale = D_ ** -0.5
    n_chunks = S_ // Q_CHUNK
    qc = q.reshape(B_, H_, n_chunks, Q_CHUNK, D_).transpose(2, 0, 1, 3, 4)
    b_ix = jnp.arange(B_)[:, None, None, None]
    h_ix = jnp.arange(H_)[None, :, None, None]

    def one_chunk(args):
        qi, ci = args
        start = ci * Q_CHUNK
        blk = start // MOBA_BLOCK
        qpos = start + jnp.arange(Q_CHUNK)
        gate = jnp.einsum('bhcd,bhnd->bhcn', qi, kmean).astype(jnp.float32)
        gate = jnp.where(jnp.arange(nb) < blk, gate, -jnp.inf)
        _, sel = lax.top_k(gate, topk)
        valid = jnp.arange(topk) < blk
        k_sel = kb[b_ix, h_ix, sel]
        v_sel = vb[b_ix, h_ix, sel]
        s_past = jnp.einsum('bhcd,bhcnkd->bhcnk', qi, k_sel).astype(jnp.float32) * scale
        s_past = jnp.where(valid[:, None], s_past, -jnp.inf)
        s_past = s_past.reshape(B_, H_, Q_CHUNK, topk * MOBA_BLOCK)
        k_own = lax.dynamic_index_in_dim(kb, blk, axis=2, keepdims=False)
        v_own = lax.dynamic_index_in_dim(vb, blk, axis=2, keepdims=False)
        s_own = jnp.einsum('bhcd,bhkd->bhck', qi, k_own).astype(jnp.float32) * scale
        kpos = blk * MOBA_BLOCK + jnp.arange(MOBA_BLOCK)
        s_own = jnp.where(kpos[None, :] <= qpos[:, None], s_own, -jnp.inf)
        p = jax.nn.softmax(jnp.concatenate([s_own, s_past], axis=-1), axis=-1).astype(v.dtype)
        p_own = p[..., :MOBA_BLOCK]
        p_past = p[..., MOBA_BLOCK:].reshape(B_, H_, Q_CHUNK, topk, MOBA_BLOCK)
        return (jnp.einsum('bhck,bhkd->bhcd', p_own, v_own)
                + jnp.einsum('bhcnk,bhcnkd->bhcd', p_past, v_sel))

    out = lax.map(one_chunk, (qc, jnp.arange(n_chunks)))
    return out.transpose(1, 2, 0, 3, 4).reshape(B_, H_, S_, D_)


def causal_multiscale_pool(u):
    B_, S_ = u.shape[0], u.shape[1]
    ug = u.reshape(B_, S_, N_POOL_GROUPS, POOL_GROUP_WIDTH).astype(jnp.float32)
    c = jnp.cumsum(ug, axis=1)
    pos = jnp.arange(1, S_ + 1, dtype=jnp.float32)
    outs = []
    for g, w in enumerate(POOL_WINDOWS):
        cg = c[:, :, g]
        lagged = jnp.pad(cg, ((0, 0), (w, 0), (0, 0)))[:, :S_]
        count = jnp.minimum(pos, float(w))[None, :, None]
        outs.append((cg - lagged) / count - ug[:, :, g])
    return jnp.stack(outs, axis=2).astype(u.dtype)


def setup_inputs(seed: int = 0) -> dict:
    key = jax.random.key(seed)
    ks = jax.random.split(key, 12)
    f32 = jnp.float32
    x = jax.random.normal(ks[0], (BATCH, SEQ, D_MODEL), f32)
    s_in = D_MODEL ** -0.5
    w_qk = jax.random.normal(ks[1], (DEPTH, D_MODEL, 2 * ATTN_WIDTH), f32) * s_in
    w_vu = jax.random.normal(ks[2], (DEPTH, D_MODEL, ATTN_WIDTH + POOL_WIDTH), f32) * (s_in * DEEPNORM_BETA)
    w_in = jnp.concatenate([w_qk, w_vu], axis=-1)
    w_pool = jax.random.normal(ks[3], (DEPTH, N_POOL_GROUPS, POOL_GROUP_WIDTH, POOL_GROUP_WIDTH), f32) * POOL_GROUP_WIDTH ** -0.5
    pool_scale = 1.0 + 0.1 * jax.random.normal(ks[4], (DEPTH, POOL_WIDTH), f32)
    w_out = jax.random.normal(ks[5], (DEPTH, D_MODEL, D_MODEL), f32) * (s_in * DEEPNORM_BETA)
    ln1_g = 1.0 + 0.02 * jax.random.normal(ks[6], (DEPTH, D_MODEL), f32)
    ln1_b = 0.02 * jax.random.normal(ks[7], (DEPTH, D_MODEL), f32)
    w_ff1 = jax.random.normal(ks[8], (DEPTH, D_MODEL, D_FF), f32) * (s_in * DEEPNORM_BETA)
    w_ff2 = jax.random.normal(ks[9], (DEPTH, D_FF, D_MODEL), f32) * (D_FF ** -0.5 * DEEPNORM_BETA)
    ln2_g = 1.0 + 0.02 * jax.random.normal(ks[10], (DEPTH, D_MODEL), f32)
    ln2_b = 0.02 * jax.random.normal(ks[11], (DEPTH, D_MODEL), f32)
    return {"x": x, "w_in": w_in, "w_pool": w_pool, "pool_scale": pool_scale,
            "w_out": w_out, "ln1_g": ln1_g, "ln1_b": ln1_b,
            "w_ff1": w_ff1, "w_ff2": w_ff2, "ln2_g": ln2_g, "ln2_b": ln2_b}


def reference(x, w_in, w_pool, pool_scale, w_out, ln1_g, ln1_b, w_ff1, w_ff2, ln2_g, ln2_b):
    B_, S_, _ = x.shape
    for l in range(DEPTH):
        proj = jnp.einsum('bsd,de->bse', x, w_in[l])
        q, k, v, u = jnp.split(proj, [ATTN_WIDTH, 2 * ATTN_WIDTH, 3 * ATTN_WIDTH], axis=-1)
        to_heads = lambda t: t.reshape(B_, S_, N_ATTN_HEADS, HEAD_DIM).transpose(0, 2, 1, 3)
        q, k, v = rope(to_heads(q)), rope(to_heads(k)), to_heads(v)
        attn = moba_attention(q, k, v).transpose(0, 2, 1, 3).reshape(B_, S_, ATTN_WIDTH)
        pooled = causal_multiscale_pool(u)
        pool = jnp.einsum('bsgc,gce->bsge', pooled, w_pool[l])
        pool = (pool * pool_scale[l].reshape(N_POOL_GROUPS, POOL_GROUP_WIDTH)).reshape(B_, S_, POOL_WIDTH)
        mix = jnp.einsum('bse,ed->bsd', jnp.concatenate([attn, pool], axis=-1), w_out[l])
        x = layer_norm(DEEPNORM_ALPHA * x + mix, ln1_g[l], ln1_b[l])
        hid = jnp.square(jax.nn.relu(jnp.einsum('bsd,df->bsf', x, w_ff1[l])))
        ff = jnp.einsum('bsf,fd->bsd', hid, w_ff2[l])
        x = layer_norm(DEEPNORM_ALPHA * x + ff, ln2_g[l], ln2_b[l])
    return x
```

```python
import numpy as np
import ml_dtypes
from contextlib import ExitStack
import concourse.bass as bass
import concourse.mybir as mybir
from concourse.bass_utils import run_bass_kernel_spmd

F32 = mybir.dt.float32
BF16 = mybir.dt.bfloat16
AF = mybir.ActivationFunctionType
ALU = mybir.AluOpType
AX = mybir.AxisListType

D = 4096
S = 4096
NB_ = 4
HALF = 2048
DFF = 16384
ALPHA = float(2.0 ** 0.25)
EPS = 1e-5
SCALE = float(128 ** -0.5)
NEG = -1.0e30
import os
A_STOP = int(os.environ.get('A_STOP', '9'))
CAST_ENGS = tuple(os.environ.get('CAST_ENGS', 'act').split(','))


class Sem:
    def __init__(self, h, name):
        self.h = h
        self.n = 0
        self.name = name


class TB:
    def __init__(self, t, const=False):
        self.t = t
        self.w = {}
        self.r = {}
        self.const = const
        self.sem = None

    def __getitem__(self, idx):
        return self.t[idx]


class Prog:
    ENG = ("pe", "act", "dve", "pool", "sp")

    def __init__(self, nc, stack, pool, tag):
        self.nc = nc
        self.stack = stack
        self.pool = pool
        self.pi = 0
        self.tag = tag
        self.q = {e: [] for e in self.ENG}
        self.esem = {e: self.sem() for e in ("pe", "act", "dve", "pool")}
        self.dma_sems = []
        self.nt = 0
        self.start_counts = {s.name: s.n for s in pool}

    def sem(self):
        s = self.pool[self.pi]
        self.pi += 1
        return s

    def sb(self, name, shape, dt, const=False):
        self.nt += 1
        return TB(self.stack.enter_context(self.nc.sbuf_tensor(f"{self.tag}_{name}", shape, dt)), const)

    def ps(self, name, shape, dt):
        return TB(self.stack.enter_context(self.nc.psum_tensor(f"{self.tag}_{name}", shape, dt)))

    def do(self, eng, fn, reads=(), writes=(), sig=True, check=True, dma_tb=None):
        waits = {}

        def add(d):
            for k, (s, v) in d.items():
                if k not in waits or waits[k][1] < v:
                    waits[k] = (s, v)

        if check:
            for tb in reads:
                add(tb.w)
            for tb in writes:
                add(tb.w)
                add(tb.r)
        sem = None
        dma = dma_tb is not None
        if sig:
            if dma:
                if dma_tb.sem is None:
                    dma_tb.sem = self.sem()
                    self.dma_sems.append(dma_tb.sem)
                sem = dma_tb.sem
                sem.n += 16
            else:
                sem = self.esem[eng]
                sem.n += 1
            for tb in writes:
                tb.w[sem.name] = (sem, sem.n)
                tb.r = {}
            for tb in reads:
                if not tb.const:
                    tb.r[sem.name] = (sem, sem.n)
        self.q[eng].append((fn, list(waits.values()), sem, dma))

    def dma(self, out, in_, reads=(), writes=(), tb=None):
        self.do("sp", lambda e: e.dma_start(out=out, in_=in_), reads, writes, dma_tb=tb)

    def simulate(self, start):
        cnt = dict(start)
        pos = {e: 0 for e in self.ENG}
        total = sum(len(v) for v in self.q.values())
        done = 0
        while done < total:
            prog = False
            for e in self.ENG:
                while pos[e] < len(self.q[e]):
                    fn, w, sig, dma = self.q[e][pos[e]]
                    if any(cnt.get(s.name, 0) < v for (s, v) in w):
                        break
                    if sig is not None:
                        cnt[sig.name] = cnt.get(sig.name, 0) + (16 if dma else 1)
                    pos[e] += 1
                    done += 1
                    prog = True
            if not prog:
                print("DEADLOCK in phase", self.tag)
                for e in self.ENG:
                    if pos[e] < len(self.q[e]):
                        fn, w, sig, dma = self.q[e][pos[e]]
                        print(" ", e, pos[e], "/", len(self.q[e]), [(s.name, v, cnt.get(s.name, 0)) for (s, v) in w if cnt.get(s.name, 0) < v])
                raise RuntimeError("deadlock")
        for s in self.pool:
            assert cnt.get(s.name, 0) == s.n, (s.name, cnt.get(s.name, 0), s.n)
        print("phase", self.tag, "sim ok", {e: len(v) for e, v in self.q.items()})

    def finish(self):
        self.q["sp"].append((None, [(s, s.n) for s in self.dma_sems if s.n > 0], None, False))
        if os.environ.get("SIMCHECK"):
            self.simulate(self.start_counts)
        nc = self.nc
        q = self.q
        with nc.Block() as block:
            def run(e, items):
                seen = {}
                for fn, w, sig, dma in items:
                    for (s, v) in w:
                        if seen.get(s.name, -1) >= v:
                            continue
                        seen[s.name] = v
                        e.wait_ge(s.h, v)
                    if fn is None:
                        continue
                    inst = fn(e)
                    if sig is not None:
                        inst.then_inc(sig.h, 16 if dma else 1)

            @block.tensor
            def _(e):
                run(e, q["pe"])

            @block.scalar
            def _(e):
                run(e, q["act"])

            @block.vector
            def _(e):
                run(e, q["dve"])

            @block.gpsimd
            def _(e):
                run(e, q["pool"])

            @block.sync
            def _(e):
                run(e, q["sp"])


class WStream:
    def __init__(self, P, name, shape, nb, srcs):
        self.P = P
        self.nb = nb
        self.srcs = srcs
        self.bufs = [P.sb(f"{name}{i}", shape, BF16) for i in range(nb)]
        for i in range(min(nb, len(srcs))):
            self._issue(i)

    def _issue(self, i):
        b = self.bufs[i % self.nb]
        self.P.dma(b[:], self.srcs[i], writes=[b], tb=b)

    def get(self, i):
        return self.bufs[i % self.nb]

    def release(self, i):
        if i + self.nb < len(self.srcs):
            self._issue(i + self.nb)


def mm_group(P, out_ap, out_tb, pairs, reads):
    n = len(pairs)
    for k, (l, r) in enumerate(pairs):
        P.do("pe", lambda e, l=l, r=r, k=k: e.matmul(out_ap, l, r, start=(k == 0), stop=(k == n - 1)),
             reads=reads, writes=[out_tb], sig=(k == n - 1), check=(k == 0))


def phase_convert(nc, pool, W, Wt, K, F, tc, tag):
    c = 512 // tc
    NB = 4
    with ExitStack() as st:
        P = Prog(nc, st, pool, tag)
        stg = [P.sb(f"stg{i}", [128, 8, 512], F32) for i in range(NB)]
        wb = [P.sb(f"wb{i}", [128, c, 8, tc], BF16) for i in range(NB)]
        chunks = [(kg, cb) for cb in range(F // 512) for kg in range(K // 1024)]

        def load(i):
            kg, cb = chunks[i]
            b = stg[i % NB]
            P.dma(b[:], W[kg * 1024:(kg + 1) * 1024, cb * 512:(cb + 1) * 512].rearrange("(k p) f -> p k f", p=128),
                  writes=[b], tb=b)

        for i in range(min(NB - 1, len(chunks))):
            load(i)
        for i, (kg, cb) in enumerate(chunks):
            if i + NB - 1 < len(chunks):
                load(i + NB - 1)
            s = stg[i % NB]
            o = wb[i % NB]
            eng = CAST_ENGS[i % len(CAST_ENGS)]
            oap = o[:].rearrange("p c k t -> p k c t")
            iap = s[:].rearrange("p k (c t) -> p k c t", t=tc)
            if eng == "act":
                P.do("act", lambda e, oap=oap, iap=iap: e.activation(out=oap, in_=iap, func=AF.Copy), reads=[s], writes=[o])
            else:
                P.do(eng, lambda e, oap=oap, iap=iap: e.tensor_copy(out=oap, in_=iap), reads=[s], writes=[o])
            P.dma(Wt[cb * c:(cb + 1) * c, :, kg * 8:(kg + 1) * 8, :].rearrange("c p k t -> p c k t"), o[:], reads=[o], tb=o)
        P.finish()


def phase_a(nc, pool, G, io):
    with ExitStack() as st:
        P = Prog(nc, st, pool, "A")
        ident, rm, kmean = G["ident"], G["rm"], G["kmean"]
        xs = [P.sb(f"xs{i}", [128, D], F32) for i in range(2)]
        xb = [P.sb(f"xb{i}", [128, D], BF16) for i in range(2)]
        xT = P.sb("xT", [128, 32, 512], BF16)
        wpool = P.sb("wpool", [128, 16, 512], BF16)
        cs = [P.sb(f"cs{i}", [128, 512], F32) for i in range(2)]
        sn = [P.sb(f"sn{i}", [128, 512], F32) for i in range(2)]
        qraw = [P.sb(f"qraw{i}", [128, 512], BF16) for i in range(2)]
        qf = [P.sb(f"qf{i}", [128, 512], F32) for i in range(2)]
        t1 = [P.sb(f"t1{i}", [128, 512], F32) for i in range(2)]
        t2 = [P.sb(f"t2{i}", [128, 512], F32) for i in range(2)]
        qo = [P.sb(f"qo{i}", [128, 512], BF16) for i in range(2)]
        vtok = [P.sb(f"vtok{i}", [128, 4, 128], BF16) for i in range(2)]
        U = [P.sb(f"U{i}", [128, 528], F32) for i in range(2)]
        SA = [P.sb(f"SA{i}", [128, 528], F32) for i in range(2)]
        SB_ = [P.sb(f"SB{i}", [128, 528], F32) for i in range(2)]
        pooled = [P.sb(f"pooled{i}", [128, 4, 512], BF16) for i in range(2)]
        po = [P.sb(f"po{i}", [128, 512], BF16) for i in range(2)]
        t16 = P.sb("t16", [128, 16], F32)
        uh = P.sb("uh", [128, 16, 16], F32)
        invc = P.sb("invc", [128, 4, 16], F32)
        psc = P.sb("psc", [128, 16], F32)
        ptr = [P.ps(f"ptr{i}", [128, 512], BF16) for i in range(2)]
        pacc = [P.ps(f"pacc{i}", [128, 512], F32) for i in range(2)]
        prot = P.ps("prot", [128, 512], F32)
        ptv = P.ps("ptv", [128, 512], BF16)
        ppool = P.ps("ppool", [128, 512], F32)

        P.dma(wpool[:], io["wpool_t"][0], writes=[wpool], tb=wpool)
        P.dma(invc[:], io["invc"], writes=[invc], tb=invc)
        P.dma(psc[:], io["psc"], writes=[psc], tb=psc)

        srcs = []
        plan = []
        for gi in range(8):
            own = gi >= 4
            fts = []
            if own:
                fts += list(range(0, 16))
            fts += list(range(16, 48))
            if own or gi == 3:
                fts += list(range(48, 64))
            if os.environ.get("A_FTS"):
                keep = set(int(v) for v in os.environ["A_FTS"].split(","))
                fts = [f_ for f_ in fts if f_ in keep]
            if os.environ.get("A_GROUPS") and gi not in set(int(v) for v in os.environ["A_GROUPS"].split(",")):
                fts = []
            plan.append(fts)
            srcs += [io["win_t"][ft] for ft in fts]
        ws = WStream(P, "w", [128, 32, 128], 3, srcs)
        wi = 0
        ev = 0
        for gi in range(8):
            if not plan[gi]:
                continue
            own = gi >= 4
            xsrc = io["xo"] if own else io["xc"]
            tok0 = (gi % 4) * 512
            slot0 = gi * 512
            c_ = cs[gi % 2]
            s_ = sn[gi % 2]
            P.dma(c_[:], io["cosT"][:, slot0:slot0 + 512], writes=[c_], tb=c_)
            P.dma(s_[:], io["sinT"][:, slot0:slot0 + 512], writes=[s_], tb=s_)

            def xload(tt):
                b = xs[tt % 2]
                P.dma(b[:], xsrc[tok0 + tt * 128: tok0 + (tt + 1) * 128, :], writes=[b], tb=b)

            xload(0)
            for tt in range(4):
                if tt + 1 < 4:
                    xload(tt + 1)
                a = xs[tt % 2]
                b = xb[tt % 2]
                if tt % 2 == 0:
                    P.do("act", lambda e, a=a, b=b: e.activation(out=b[:], in_=a[:], func=AF.Copy), reads=[a], writes=[b])
                else:
                    P.do("pool", lambda e, a=a, b=b: e.tensor_copy(out=b[:], in_=a[:]), reads=[a], writes=[b])
                for k4 in range(8 if A_STOP >= 2 else 0):
                    p_ = ptr[k4 % 2]
                    for j in range(4):
                        kc = k4 * 4 + j
                        P.do("pe", lambda e, p_=p_, b=b, kc=kc, j=j: e.transpose(p_[:, j * 128:(j + 1) * 128], b[:, kc * 128:(kc + 1) * 128], ident[:]),
                             reads=[b, ident], writes=[p_], sig=(j == 3), check=(j == 0))
                    P.do("dve", lambda e, p_=p_, k4=k4, tt=tt: e.tensor_copy(out=xT[:, k4 * 4:(k4 + 1) * 4, tt * 128:(tt + 1) * 128],
                                                                          in_=p_[:].rearrange("p (a t) -> p a t", t=128)),
                         reads=[p_], writes=[xT])
            for ft in (plan[gi] if A_STOP >= 3 else []):
                w = ws.get(wi)
                pa = pacc[ev % 2]
                mm_group(P, pa[:], pa, [(w[:, kc, :], xT[:, kc, :]) for kc in range(32)], [w, xT])
                ws.release(wi)
                wi += 1
                b2 = ev % 2
                if A_STOP < 4:
                    ev += 1
                    continue
                if ft < 32:
                    h = ft % 16
                    isk = ft >= 16
                    qr, a1, a2, o_ = qraw[b2], t1[b2], t2[b2], qo[b2]
                    qf_ = qf[b2]
                    P.do("act", lambda e, qf_=qf_, pa=pa: e.activation(out=qf_[:], in_=pa[:], func=AF.Copy), reads=[pa], writes=[qf_])
                    P.do("pool", lambda e, qr=qr, qf_=qf_: e.tensor_copy(out=qr[:], in_=qf_[:]), reads=[qf_], writes=[qr])
                    P.do("pe", lambda e, qr=qr: e.matmul(prot[:], rm[:], qr[:], start=True, stop=True), reads=[qr, rm], writes=[prot])
                    P.do("dve", lambda e, a1=a1, qf_=qf_, c_=c_: e.tensor_tensor(out=a1[:], in0=qf_[:], in1=c_[:], op=ALU.mult), reads=[qf_, c_], writes=[a1])
                    P.do("act", lambda e, a2=a2: e.activation(out=a2[:], in_=prot[:], func=AF.Copy), reads=[prot], writes=[a2])
                    P.do("dve", lambda e, a2=a2, s_=s_: e.tensor_tensor(out=a2[:], in0=a2[:], in1=s_[:], op=ALU.mult), reads=[s_], writes=[a2])
                    P.do("pool", lambda e, a1=a1, a2=a2: e.tensor_tensor(out=a1[:], in0=a1[:], in1=a2[:], op=ALU.add), reads=[a2], writes=[a1])
                    if isk:
                        P.do("dve", lambda e, a1=a1, h=h, gi=gi: e.tensor_reduce(out=kmean[:, h, gi * 2:gi * 2 + 2],
                                                                                in_=a1[:].rearrange("p (b t) -> p b t", t=256),
                                                                                axis=AX.X, op=ALU.add), reads=[a1], writes=[kmean])
                    P.do("act", lambda e, a1=a1, o_=o_: e.activation(out=o_[:], in_=a1[:], func=AF.Copy), reads=[a1], writes=[o_])
                    if isk:
                        P.dma(io["kT"][h][:, slot0:slot0 + 512], o_[:], reads=[o_], tb=o_)
                    else:
                        P.dma(io["qT"][h][:, tok0:tok0 + 512], o_[:], reads=[o_], tb=o_)
                elif ft < 48:
                    h = ft - 32
                    qr, vt = qraw[b2], vtok[b2]
                    P.do("act", lambda e, qr=qr, pa=pa: e.activation(out=qr[:], in_=pa[:], func=AF.Copy), reads=[pa], writes=[qr])
                    for j in range(4):
                        P.do("pe", lambda e, qr=qr, j=j: e.transpose(ptv[:, j * 128:(j + 1) * 128], qr[:, j * 128:(j + 1) * 128], ident[:]),
                             reads=[qr, ident], writes=[ptv], sig=(j == 3), check=(j == 0))
                    P.do("dve", lambda e, vt=vt: e.tensor_copy(out=vt[:], in_=ptv[:].rearrange("p (a t) -> p a t", t=128)), reads=[ptv], writes=[vt])
                    P.dma(io["V"][h][:, gi * 4:(gi + 1) * 4, :], vt[:], reads=[vt], tb=vt)
                else:
                    cti = ft - 48
                    pg, ct = cti // 4, cti % 4
                    wdw = 2 << pg
                    u_, sa, sb2 = U[b2], SA[b2], SB_[b2]
                    pl = pooled[pg % 2]
                    P.do("act", lambda e, u_=u_, pa=pa: e.activation(out=u_[:, 16:528], in_=pa[:], func=AF.Copy), reads=[pa], writes=[u_])
                    if own:
                        P.do("pool", lambda e, u_=u_, cti=cti: e.tensor_copy(out=u_[:, 0:16], in_=uh[:, cti, :]), reads=[uh, u_], writes=[u_])
                    P.do("pool", lambda e, u_=u_, cti=cti: e.tensor_copy(out=uh[:, cti, :], in_=u_[:, 512:528]), reads=[u_], writes=[uh])
                    if own:
                        src, dst, oth = u_, sa, sb2
                        sh = 1
                        lo = 1
                        while sh < wdw:
                            eng = "dve" if (sh in (1, 4)) else "pool"
                            P.do(eng, lambda e, src=src, dst=dst, sh=sh, lo=lo: e.tensor_tensor(out=dst[:, lo:528], in0=src[:, lo:528], in1=src[:, lo - sh:528 - sh], op=ALU.add),
                                 reads=[src], writes=[dst])
                            src = dst
                            dst = oth if dst is sa else sa
                            oth = sa if dst is sb2 else sb2
                            sh *= 2
                            lo = 2 * sh - 1
                        sw = src
                        P.do("dve", lambda e, sw=sw, u_=u_, pl=pl, ct=ct, wdw=wdw: e.scalar_tensor_tensor(out=pl[:, ct, :], in0=sw[:, 16:528], scalar=1.0 / wdw, in1=u_[:, 16:528],
                                                                                                    op0=ALU.mult, op1=ALU.subtract), reads=[sw, u_], writes=[pl])
                        if gi == 4:
                            P.do("dve", lambda e, sw=sw, pg=pg: e.tensor_tensor(out=t16[:], in0=sw[:, 16:32], in1=invc[:, pg, :], op=ALU.mult), reads=[sw, invc], writes=[t16])
                            P.do("dve", lambda e, u_=u_, pl=pl, ct=ct: e.tensor_tensor(out=pl[:, ct, 0:16], in0=t16[:], in1=u_[:, 16:32], op=ALU.subtract), reads=[t16, u_], writes=[pl])
                        if ct == 3:
                            for et in range(4):
                                mm_group(P, ppool[:], ppool, [(wpool[:, pg * 4 + c2, et * 128:(et + 1) * 128], pl[:, c2, :]) for c2 in range(4)], [wpool, pl])
                                o_ = po[et % 2]
                                P.do("act", lambda e, o_=o_, pg=pg, et=et: e.activation(out=o_[:], in_=ppool[:], func=AF.Copy, scale=psc[:, pg * 4 + et:pg * 4 + et + 1]),
                                     reads=[ppool, psc], writes=[o_])
                                P.dma(io["mixT"][16 + pg * 4 + et][:, tok0:tok0 + 512], o_[:], reads=[o_], tb=o_)
                ev += 1
        P.finish()


def phase_b(nc, pool, G, io):
    with ExitStack() as st:
        P = Prog(nc, st, pool, "B")
        ident, kmean = G["ident"], G["kmean"]
        qT = [P.sb(f"qT{i}", [128, HALF], BF16) for i in range(2)]
        kT = [P.sb(f"kT{i}", [128, S], BF16) for i in range(2)]
        Vb = [P.sb(f"Vb{i}", [128, 32, 132], BF16) for i in range(2)]
        aT = [P.sb(f"aT{i}", [128, HALF], BF16) for i in range(2)]
        kmb = P.sb("kmb", [128, 16], BF16)
        gbias = P.sb("gbias", [128, 8, 2, 16], F32)
        tri = P.sb("tri", [128, 128], BF16)
        gm = P.sb("gm", [128, 2, 16], F32)
        top8 = P.sb("top8", [128, 2, 8], F32)
        thr = P.sb("thr", [128, 2], F32)
        sel = [P.sb(f"sel{i}", [128, 2, 16], F32) for i in range(2)]
        pTs = [P.sb(f"pTs{i}", [128, 512], BF16) for i in range(3)]
        acc = [P.sb(f"acc{i}", [128, 2, 132], F32) for i in range(2)]
        rec = P.sb("rec", [128, 2], F32)
        osb = [P.sb(f"osb{i}", [128, 2, 132], F32) for i in range(2)]
        ab = [P.sb(f"ab{i}", [128, 2, 128], BF16) for i in range(2)]
        pS = [P.ps(f"pS{i}", [128, 512], F32) for i in range(2)]
        pO = [P.ps(f"pO{i}", [128, 2, 256], F32) for i in range(2)]
        pG = P.ps("pG", [128, 2, 16], F32)
        pT = P.ps("pT", [128, 512], BF16)

        P.dma(gbias[:], io["gbias"], writes=[gbias], tb=gbias)
        P.dma(tri[:], io["tri"], writes=[tri], tb=tri)
        for i in range(2):
            P.do("pool", lambda e, i=i: e.memset(Vb[i][:, :, 128:129], 1.0), writes=[Vb[i]])

        def hload(h):
            b = h % 2
            P.dma(qT[b][:], io["qT"][h], writes=[qT[b]], tb=qT[b])
            P.dma(kT[b][:], io["kT"][h], writes=[kT[b]], tb=kT[b])
            P.dma(Vb[b][:, :, 0:128], io["V"][h], writes=[Vb[b]], tb=Vb[b])

        hload(0)
        un = 0
        for h in range(16):
            if h + 1 < 16:
                hload(h + 1)
            q_, k_, v_, a_ = qT[h % 2], kT[h % 2], Vb[h % 2], aT[h % 2]
            P.do("act", lambda e, h=h: e.activation(out=kmb[:], in_=kmean[:, h, :], func=AF.Copy, scale=1.0 / 256.0), reads=[kmean], writes=[kmb])
            for qb in range(8):
                ac = acc[qb % 2]
                sl = sel[qb % 2]
                q0 = qb * 256
                for a in range(2):
                    P.do("pe", lambda e, a=a, q_=q_, q0=q0: e.matmul(pG[:, a, :], q_[:, q0 + a * 128:q0 + (a + 1) * 128], kmb[:], start=True, stop=True),
                         reads=[q_, kmb], writes=[pG], sig=(a == 1), check=(a == 0))
                P.do("dve", lambda e: e.tensor_copy(out=gm[:], in_=pG[:]), reads=[pG], writes=[gm])
                P.do("dve", lambda e, qb=qb: e.tensor_tensor(out=gm[:], in0=gm[:], in1=gbias[:, qb, :, :], op=ALU.add), reads=[gbias], writes=[gm])
                for a in range(2):
                    P.do("dve", lambda e, a=a: e.max(out=top8[:, a, :], in_=gm[:, a, :]), reads=[gm], writes=[top8])
                P.do("dve", lambda e: e.tensor_scalar(out=thr[:], in0=top8[:, :, 2], scalar1=-1.0e29, scalar2=None, op0=ALU.max), reads=[top8], writes=[thr])
                for a in range(2):
                    P.do("dve", lambda e, a=a, sl=sl: e.tensor_scalar(out=sl[:, a, :], in0=gm[:, a, :], scalar1=thr[:, a:a + 1], scalar2=None, op0=ALU.is_ge),
                         reads=[gm, thr], writes=[sl])
                so = 8 + qb
                for s in [so] + list(range(0, so)):
                    ps_ = pS[un % 2]
                    po_ = pO[un % 2]
                    pt = pTs[un % 3]
                    k0 = s * 256
                    if s == so:
                        P.do("pe", lambda e, ps_=ps_, k_=k_, q_=q_, k0=k0, q0=q0: e.matmul(ps_[:, 0:256], k_[:, k0:k0 + 128], q_[:, q0:q0 + 256], start=True, stop=True),
                             reads=[k_, q_], writes=[ps_], sig=False)
                        P.do("pe", lambda e, ps_=ps_, k_=k_, q_=q_, k0=k0, q0=q0: e.matmul(ps_[:, 256:384], k_[:, k0 + 128:k0 + 256], q_[:, q0 + 128:q0 + 256], start=True, stop=True),
                             reads=[k_, q_], writes=[ps_], check=False)
                        P.do("act", lambda e, ps_=ps_, pt=pt: e.activation(out=pt[:, 0:384], in_=ps_[:, 0:384], func=AF.Exp, scale=SCALE), reads=[ps_], writes=[pt])
                        P.do("pool", lambda e, pt=pt: e.tensor_tensor(out=pt[:, 0:128], in0=pt[:, 0:128], in1=tri[:], op=ALU.mult), reads=[tri], writes=[pt])
                        P.do("pool", lambda e, pt=pt: e.tensor_tensor(out=pt[:, 256:384], in0=pt[:, 256:384], in1=tri[:], op=ALU.mult), reads=[tri], writes=[pt])
                        P.do("pe", lambda e, po_=po_, pt=pt, v_=v_, s=s: e.matmul(po_[:, 0, 0:129], pt[:, 0:128], v_[:, s * 2, 0:129], start=True, stop=True),
                             reads=[pt, v_], writes=[po_], sig=False)
                        P.do("pe", lambda e, po_=po_, pt=pt, v_=v_, s=s: e.matmul(po_[:, 1, 0:129], pt[:, 128:256], v_[:, s * 2, 0:129], start=True, stop=False),
                             reads=[pt, v_], writes=[po_], sig=False, check=False)
                        P.do("pe", lambda e, po_=po_, pt=pt, v_=v_, s=s: e.matmul(po_[:, 1, 0:129], pt[:, 256:384], v_[:, s * 2 + 1, 0:129], start=False, stop=True),
                             reads=[pt, v_], writes=[po_], check=False)
                        P.do("dve", lambda e, ac=ac, po_=po_: e.tensor_copy(out=ac[:, :, 0:129], in_=po_[:, :, 0:129]), reads=[po_], writes=[ac])
                    else:
                        for kt in range(2):
                            P.do("pe", lambda e, ps_=ps_, k_=k_, q_=q_, k0=k0, q0=q0, kt=kt: e.matmul(ps_[:, kt * 256:(kt + 1) * 256], k_[:, k0 + kt * 128:k0 + (kt + 1) * 128],
                                                                                                  q_[:, q0:q0 + 256], start=True, stop=True),
                                 reads=[k_, q_], writes=[ps_], sig=(kt == 1), check=(kt == 0))
                        P.do("act", lambda e, ps_=ps_, pt=pt: e.activation(out=pt[:], in_=ps_[:], func=AF.Exp, scale=SCALE), reads=[ps_], writes=[pt])
                        for a in range(2):
                            for kt in range(2):
                                P.do("pe", lambda e, po_=po_, pt=pt, v_=v_, s=s, a=a, kt=kt: e.matmul(po_[:, a, 0:129], pt[:, kt * 256 + a * 128:kt * 256 + (a + 1) * 128],
                                                                                              v_[:, s * 2 + kt, 0:129], start=(kt == 0), stop=(kt == 1)),
                                     reads=[pt, v_], writes=[po_], sig=(a == 1 and kt == 1), check=(a == 0 and kt == 0))
                        ob = osb[un % 2]
                        for a in range(2):
                            P.do("act", lambda e, ob=ob, po_=po_, sl=sl, a=a, s=s: e.activation(out=ob[:, a, 0:129], in_=po_[:, a, 0:129], func=AF.Copy, scale=sl[:, a, s:s + 1]),
                                 reads=[po_, sl], writes=[ob])
                        P.do("dve", lambda e, ac=ac, ob=ob: e.tensor_tensor(out=ac[:, :, 0:129], in0=ac[:, :, 0:129], in1=ob[:, :, 0:129], op=ALU.add),
                             reads=[ob], writes=[ac])
                    un += 1
                ab_ = ab[qb % 2]
                P.do("dve", lambda e, ac=ac: e.reciprocal(out=rec[:], in_=ac[:, :, 128]), reads=[ac], writes=[rec])
                for a in range(2):
                    P.do("dve", lambda e, ac=ac, ab_=ab_, a=a: e.tensor_scalar(out=ab_[:, a, :], in0=ac[:, a, 0:128], scalar1=rec[:, a:a + 1], scalar2=None, op0=ALU.mult),
                         reads=[ac, rec], writes=[ab_])
                for a in range(2):
                    c0 = (qb % 2) * 256 + a * 128
                    P.do("pe", lambda e, ab_=ab_, a=a, c0=c0: e.transpose(pT[:, c0:c0 + 128], ab_[:, a, :], ident[:]),
                         reads=[ab_, ident], writes=[pT], sig=(a == 1), check=(a == 0))
                if qb % 2 == 1:
                    P.do("act", lambda e, a_=a_, qb=qb: e.activation(out=a_[:, (qb - 1) * 256:(qb + 1) * 256], in_=pT[:], func=AF.Copy), reads=[pT], writes=[a_])
            P.dma(io["mixT"][h], a_[:], reads=[a_], tb=a_)
        P.finish()


def phase_de(nc, pool, G, io):
    TT = 256
    NG = HALF // TT
    with ExitStack() as st:
        P = Prog(nc, st, pool, "E")
        ident = G["ident"]
        hid = P.sb("hid", [128, 128, TT], BF16)
        y = P.sb("y", [128, 2, D], F32)
        hT = P.sb("hT", [128, 32, TT], BF16)
        xch = [P.sb(f"xch{i}", [128, 512], F32) for i in range(2)]
        gch = [P.sb(f"gch{i}", [128, 512], F32) for i in range(2)]
        bch = [P.sb(f"bch{i}", [128, 512], F32) for i in range(2)]
        rl = [P.sb(f"rl{i}", [128, TT], F32) for i in range(2)]
        evt = [P.sb(f"evt{i}", [128, 256], F32) for i in range(2)]
        hb = [P.sb(f"hb{i}", [128, 512], BF16) for i in range(2)]
        stats = P.sb("stats", [128, 8, 6], F32)
        mv = P.sb("mv", [128, 2], F32)
        sd = P.sb("sd", [128, 1], F32)
        rstd = P.sb("rstd", [128, 1], F32)
        nmr = P.sb("nmr", [128, 1], F32)
        epsb = P.sb("epsb", [128, 1], F32)
        pacc = [P.ps(f"pacc{i}", [128, 512], F32) for i in range(2)]
        pac2 = [P.ps(f"pac2{i}", [128, 512], F32) for i in range(4)]
        pT = P.ps("pT", [128, 512], BF16)
        P.do("pool", lambda e: e.memset(epsb[:], EPS), writes=[epsb])

        msrc = []
        w1src = []
        for tg in range(NG):
            msrc += [io["wout_t"][cb] for cb in range(16)]
            msrc += [io["wff2_t"][cb][:, kg * 32:(kg + 1) * 32, :] for cb in range(16) for kg in range(4)]
            w1src += [io["wff1_t"][ft] for ft in range(128)]
        wm = WStream(P, "wm", [128, 32, 256], 2, msrc)
        w1 = WStream(P, "w1", [128, 32, 128], 3, w1src)
        mi = 0
        fi = 0
        ev = 0
        ci = 0

        def layer_norm(tt, gsrc, bsrc, tok0, final):
            nonlocal ci
            for c in range(8):
                P.do("dve", lambda e, c=c: e.bn_stats(out=stats[:, c, :], in_=y[:, tt, c * 512:(c + 1) * 512]), reads=[y], writes=[stats])
            P.do("dve", lambda e: e.bn_aggr(out=mv[:], in_=stats[:].rearrange("p a b -> p (a b)")), reads=[stats], writes=[mv])
            P.do("act", lambda e: e.activation(out=sd[:], in_=mv[:, 1:2], func=AF.Sqrt, bias=epsb[:], scale=1.0), reads=[mv, epsb], writes=[sd])
            P.do("dve", lambda e: e.reciprocal(out=rstd[:], in_=sd[:]), reads=[sd], writes=[rstd])
            P.do("dve", lambda e: e.scalar_tensor_tensor(out=nmr[:], in0=mv[:, 0:1], scalar=-1.0, in1=rstd[:], op0=ALU.mult, op1=ALU.mult), reads=[mv, rstd], writes=[nmr])
            for c in range(8):
                g_, b_ = gch[ci % 2], bch[ci % 2]
                ci += 1
                P.dma(g_[:], gsrc[:, c * 512:(c + 1) * 512].partition_broadcast(128), writes=[g_], tb=g_)
                P.dma(b_[:], bsrc[:, c * 512:(c + 1) * 512].partition_broadcast(128), writes=[b_], tb=b_)
                ysl = y[:, tt, c * 512:(c + 1) * 512]
                P.do("dve", lambda e, ysl=ysl: e.tensor_scalar(out=ysl, in0=ysl, scalar1=rstd[:, 0:1], scalar2=nmr[:, 0:1], op0=ALU.mult, op1=ALU.add), reads=[rstd, nmr], writes=[y])
                P.do("pool", lambda e, ysl=ysl, g_=g_: e.tensor_tensor(out=ysl, in0=ysl, in1=g_[:], op=ALU.mult), reads=[g_], writes=[y])
                P.do("pool", lambda e, ysl=ysl, b_=b_: e.tensor_tensor(out=ysl, in0=ysl, in1=b_[:], op=ALU.add), reads=[b_], writes=[y])
                if not final:
                    hb_ = hb[c % 2]
                    P.do("act", lambda e, ysl=ysl, hb_=hb_: e.activation(out=hb_[:], in_=ysl, func=AF.Copy), reads=[y], writes=[hb_])
                    for j in range(4):
                        P.do("pe", lambda e, hb_=hb_, j=j: e.transpose(pT[:, j * 128:(j + 1) * 128], hb_[:, j * 128:(j + 1) * 128], ident[:]),
                             reads=[hb_, ident], writes=[pT], sig=(j == 3), check=(j == 0))
                    P.do("dve", lambda e, c=c: e.tensor_copy(out=hT[:, c * 4:(c + 1) * 4, tt * 128:(tt + 1) * 128], in_=pT[:].rearrange("p (a t) -> p a t", t=128)),
                         reads=[pT], writes=[hT])
            if final:
                P.dma(io["out"][tok0 + tt * 128:tok0 + (tt + 1) * 128, :], y[:, tt, :], reads=[y], tb=y)

        for tg in range(NG):
            tok0 = tg * TT
            P.dma(hid[:, 0:32, :], io["mixT"][:, :, tok0:tok0 + TT].rearrange("k p t -> p k t"), writes=[hid], tb=hid)
            for cb in range(16):
                w = wm.get(mi)
                for tt in range(2):
                    if cb % 2 == 0:
                        xc_ = xch[(cb // 2 * 2 + tt) % 2]
                        P.dma(xc_[:], io["xo"][tok0 + tt * 128:tok0 + (tt + 1) * 128, (cb // 2) * 512:(cb // 2 + 1) * 512], writes=[xc_], tb=xc_)
                    xc_ = xch[(cb // 2 * 2 + tt) % 2]
                    pa = pacc[ev % 2]
                    ev += 1
                    mm_group(P, pa[:, 0:256], pa, [(hid[:, kc, tt * 128:(tt + 1) * 128], w[:, kc, :]) for kc in range(32)], [hid, w])
                    et_ = evt[ev % 2]
                    P.do("act", lambda e, et_=et_, pa=pa: e.activation(out=et_[:], in_=pa[:, 0:256], func=AF.Copy), reads=[pa], writes=[et_])
                    P.do("dve", lambda e, tt=tt, cb=cb, xc_=xc_, et_=et_: e.scalar_tensor_tensor(out=y[:, tt, cb * 256:(cb + 1) * 256], in0=xc_[:, (cb % 2) * 256:(cb % 2 + 1) * 256],
                                                                                            scalar=ALPHA, in1=et_[:], op0=ALU.mult, op1=ALU.add),
                         reads=[xc_, et_], writes=[y])
                wm.release(mi)
                mi += 1
            for tt in range(2):
                layer_norm(tt, io["ln1_g"], io["ln1_b"], tok0, False)
            for ft in range(128):
                w = w1.get(fi)
                pa = pacc[ev % 2]
                r_ = rl[ev % 2]
                ev += 1
                mm_group(P, pa[:, 0:TT], pa, [(w[:, kc, :], hT[:, kc, :]) for kc in range(32)], [w, hT])
                w1.release(fi)
                fi += 1
                P.do("act", lambda e, r_=r_, pa=pa: e.activation(out=r_[:], in_=pa[:, 0:TT], func=AF.Relu), reads=[pa], writes=[r_])
                P.do("pool", lambda e, r_=r_, ft=ft: e.tensor_tensor(out=hid[:, ft, :], in0=r_[:], in1=r_[:], op=ALU.mult), reads=[r_], writes=[hid])
            for cb in range(16):
                for kg in range(4):
                    w = wm.get(mi)
                    for tt in range(2):
                        pa = pac2[(cb % 2) * 2 + tt]
                        n = 32
                        for k in range(n):
                            kc = kg * 32 + k
                            P.do("pe", lambda e, pa=pa, tt=tt, kc=kc, k=k, w=w, kg=kg: e.matmul(pa[:, 0:256], hid[:, kc, tt * 128:(tt + 1) * 128], w[:, k, :],
                                                                                           start=(kg == 0 and k == 0), stop=(kg == 3 and k == 31)),
                                 reads=[hid, w], writes=[pa], sig=(k == n - 1), check=(k == 0))
                    wm.release(mi)
                    mi += 1
                for tt in range(2):
                    pa = pac2[(cb % 2) * 2 + tt]
                    ysl = y[:, tt, cb * 256:(cb + 1) * 256]
                    et_ = evt[tt]
                    P.do("act", lambda e, et_=et_, pa=pa: e.activation(out=et_[:], in_=pa[:, 0:256], func=AF.Copy), reads=[pa], writes=[et_])
                    P.do("dve", lambda e, ysl=ysl, et_=et_: e.scalar_tensor_tensor(out=ysl, in0=ysl, scalar=ALPHA, in1=et_[:], op0=ALU.mult, op1=ALU.add),
                         reads=[et_], writes=[y])
            for tt in range(2):
                layer_norm(tt, io["ln2_g"], io["ln2_b"], tok0, True)
        P.finish()


def build(phases=None, dbg=False):
    nc = bass.Bass("TRN2", target_bir_lowering=False)

    def din(name, shape, dt=F32):
        return nc.dram_tensor(name, shape, dt, kind="ExternalInput").ap()

    def scr(name, shape, dt=BF16):
        kind = "ExternalOutput" if (dbg and name in ("qT", "kT", "V", "mixT")) else "Internal"
        return nc.dram_tensor(name, shape, dt, kind=kind).ap()

    io = {}
    io["xo"] = din("xo", [HALF, D])
    io["xc"] = din("xc", [HALF, D])
    w_in = din("w_in", [D, 8192])
    w_pool = din("w_pool", [2048, 512])
    w_out = din("w_out", [D, D])
    w_ff1 = din("w_ff1", [D, DFF])
    w_ff2 = din("w_ff2", [DFF, D])
    for n in ("ln1_g", "ln1_b", "ln2_g", "ln2_b"):
        io[n] = din(n, [1, D])
    io["cosT"] = din("cosT", [128, S])
    io["sinT"] = din("sinT", [128, S])
    io["psc"] = din("psc", [128, 16])
    io["invc"] = din("invc", [128, 4, 16])
    io["gbias"] = din("gbias", [128, 8, 2, 16])
    c_ident = din("ident", [128, 128], BF16)
    c_rm = din("rm", [128, 128], BF16)
    io["tri"] = din("tri", [128, 128], BF16)
    io["out"] = nc.dram_tensor("out", [HALF, D], F32, kind="ExternalOutput").ap()
    io["win_t"] = scr("win_t", [64, 128, 32, 128])
    io["wff1_t"] = scr("wff1_t", [128, 128, 32, 128])
    io["wout_t"] = scr("wout_t", [16, 128, 32, 256])
    io["wff2_t"] = scr("wff2_t", [16, 128, 128, 256])
    io["wpool_t"] = scr("wpool_t", [1, 128, 16, 512])
    io["qT"] = scr("qT", [16, 128, HALF])
    io["kT"] = scr("kT", [16, 128, S])
    io["V"] = scr("V", [16, 128, 32, 128])
    io["mixT"] = scr("mixT", [32, 128, HALF])

    with ExitStack() as gst:
        pool = [Sem(gst.enter_context(nc.semaphore(f"gs{i}")), f"gs{i}") for i in range(96)]
        G = {}
        G["ident"] = TB(gst.enter_context(nc.sbuf_tensor("g_ident", [128, 128], BF16)), const=True)
        G["rm"] = TB(gst.enter_context(nc.sbuf_tensor("g_rm", [128, 128], BF16)), const=True)
        G["kmean"] = TB(gst.enter_context(nc.sbuf_tensor("g_kmean", [128, 16, 16], F32)))
        with ExitStack() as st:
            P = Prog(nc, st, pool, "I")
            P.dma(G["ident"][:], c_ident, tb=G["ident"])
            P.dma(G["rm"][:], c_rm, tb=G["rm"])
            P.finish()
        ph = phases or ("W0", "W1", "A", "W2", "W3", "W4", "B", "E")
        if "W0" in ph:
            phase_convert(nc, pool, w_in, io["win_t"], D, 8192, 128, "W0")
        if "W1" in ph:
            phase_convert(nc, pool, w_pool, io["wpool_t"], 2048, 512, 512, "W1")
        if "A" in ph:
            phase_a(nc, pool, G, io)
        if "W2" in ph:
            phase_convert(nc, pool, w_out, io["wout_t"], D, D, 256, "W2")
        if "W3" in ph:
            phase_convert(nc, pool, w_ff1, io["wff1_t"], D, DFF, 128, "W3")
        if "W4" in ph:
            phase_convert(nc, pool, w_ff2, io["wff2_t"], DFF, D, 256, "W4")
        if "B" in ph:
            phase_b(nc, pool, G, io)
        if "E" in ph:
            phase_de(nc, pool, G, io)
    return nc


_NC = None
_DEBUG_RET_MAPS = False


def _consts():
    bf = ml_dtypes.bfloat16
    ident = np.eye(128, dtype=np.float32).astype(bf)
    rm = np.zeros((128, 128), np.float32)
    for m in range(64):
        rm[m + 64, m] = -1.0
        rm[m, m + 64] = 1.0
    rm = rm.astype(bf)
    kk = np.arange(128)[:, None]
    qq = np.arange(128)[None, :]
    tri = (kk <= qq).astype(np.float32).astype(bf)
    inv_freq = (1.0 / (np.float32(10000.0) ** (np.arange(0, 128, 2, dtype=np.float32) / np.float32(128)))).astype(np.float32)
    return ident, rm, tri, inv_freq


def kernel(x, w_in, w_pool, pool_scale, w_out, ln1_g, ln1_b, w_ff1, w_ff2, ln2_g, ln2_b):
    global _NC
    if _NC is None:
        _NC = build()
    nc = _NC
    ident, rm, tri, inv_freq = _consts()
    x = np.asarray(x, np.float32)
    f = lambda a: np.ascontiguousarray(np.asarray(a, np.float32))
    shared = {
        "w_in": f(w_in[0]), "w_pool": f(np.asarray(w_pool[0]).reshape(2048, 512)), "w_out": f(w_out[0]),
        "w_ff1": f(w_ff1[0]), "w_ff2": f(w_ff2[0]),
        "ln1_g": f(ln1_g), "ln1_b": f(ln1_b), "ln2_g": f(ln2_g), "ln2_b": f(ln2_b),
        "psc": f(np.asarray(pool_scale[0]).reshape(16, 128).T),
        "ident": ident, "rm": rm, "tri": tri,
    }
    in_maps = []
    for c in range(8):
        b, half = c // 2, c % 2
        xo = f(x[b, half * HALF:(half + 1) * HALF])
        xc = f(x[b, 0:HALF]) if half == 1 else np.zeros((HALF, D), np.float32)
        pos = np.concatenate([np.arange(HALF), half * HALF + np.arange(HALF)]).astype(np.float32)
        ang = pos[:, None] * inv_freq[None, :]
        cosv = np.cos(ang).astype(np.float32)
        sinv = np.sin(ang).astype(np.float32)
        cosT = np.ascontiguousarray(np.concatenate([cosv, cosv], axis=1).T)
        sinT = np.ascontiguousarray(np.concatenate([sinv, sinv], axis=1).T)
        gb = np.full((8, 16), NEG, np.float32)
        for i in range(8):
            for s in range(16):
                if s < 8 + i and (half == 1 or s >= 8):
                    gb[i, s] = 0.0
        gbias = np.ascontiguousarray(np.broadcast_to(gb[None, :, None, :], (128, 8, 2, 16))).astype(np.float32)
        invc = np.zeros((4, 16), np.float32)
        for g, w in enumerate((2, 4, 8, 16)):
            for t in range(16):
                invc[g, t] = 1.0 / (min(t + 1, w) if half == 0 else w)
        invc = np.ascontiguousarray(np.broadcast_to(invc[None], (128, 4, 16))).astype(np.float32)
        m = dict(shared)
        m.update({"xo": xo, "xc": xc, "cosT": cosT, "sinT": sinT, "gbias": gbias, "invc": invc})
        in_maps.append(m)
    if _DEBUG_RET_MAPS:
        return in_maps
    res = run_bass_kernel_spmd(nc, in_maps, core_ids=list(range(8)))
    out = np.empty((NB_, S, D), np.float32)
    for c in range(8):
        b, half = c // 2, c % 2
        out[b, half * HALF:(half + 1) * HALF] = np.asarray(res.results[c]["out"], np.float32)
    return out
```

```python
import numpy as np
import ml_dtypes
from contextlib import ExitStack
import concourse.bass as bass
import concourse.mybir as mybir
from concourse.bass_utils import run_bass_kernel_spmd

F32 = mybir.dt.float32
BF16 = mybir.dt.bfloat16
AF = mybir.ActivationFunctionType
ALU = mybir.AluOpType
AX = mybir.AxisListType

D = 4096
S = 4096
NB_ = 4
HALF = 2048
DFF = 16384
ALPHA = float(2.0 ** 0.25)
EPS = 1e-5
SCALE = float(128 ** -0.5)
NEG = -1.0e30
import os
A_STOP = int(os.environ.get('A_STOP', '9'))
CAST_ENGS = tuple(os.environ.get('CAST_ENGS', 'act').split(','))


class Sem:
    def __init__(self, h, name):
        self.h = h
        self.n = 0
        self.name = name


class TB:
    def __init__(self, t, const=False):
        self.t = t
        self.w = {}
        self.r = {}
        self.const = const
        self.sem = None

    def __getitem__(self, idx):
        return self.t[idx]


class Prog:
    ENG = ("pe", "act", "dve", "pool", "sp")

    def __init__(self, nc, stack, pool, tag):
        self.nc = nc
        self.stack = stack
        self.pool = pool
        self.pi = 0
        self.tag = tag
        self.q = {e: [] for e in self.ENG}
        self.esem = {e: self.sem() for e in ("pe", "act", "dve", "pool")}
        self.dma_sems = []
        self.nt = 0
        self.start_counts = {s.name: s.n for s in pool}

    def sem(self):
        s = self.pool[self.pi]
        self.pi += 1
        return s

    def sb(self, name, shape, dt, const=False):
        self.nt += 1
        return TB(self.stack.enter_context(self.nc.sbuf_tensor(f"{self.tag}_{name}", shape, dt)), const)

    def ps(self, name, shape, dt):
        return TB(self.stack.enter_context(self.nc.psum_tensor(f"{self.tag}_{name}", shape, dt)))

    def do(self, eng, fn, reads=(), writes=(), sig=True, check=True, dma_tb=None):
        waits = {}

        def add(d):
            for k, (s, v) in d.items():
                if k not in waits or waits[k][1] < v:
                    waits[k] = (s, v)

        if check:
            for tb in reads:
                add(tb.w)
            for tb in writes:
                add(tb.w)
                add(tb.r)
        sem = None
        dma = dma_tb is not None
        if sig:
            if dma:
                if dma_tb.sem is None:
                    dma_tb.sem = self.sem()
                    self.dma_sems.append(dma_tb.sem)
                sem = dma_tb.sem
                sem.n += 16
            else:
                sem = self.esem[eng]
                sem.n += 1
            for tb in writes:
                tb.w[sem.name] = (sem, sem.n)
                tb.r = {}
            for tb in reads:
                if not tb.const:
                    tb.r[sem.name] = (sem, sem.n)
        self.q[eng].append((fn, list(waits.values()), sem, dma))

    def dma(self, out, in_, reads=(), writes=(), tb=None):
        self.do("sp", lambda e: e.dma_start(out=out, in_=in_), reads, writes, dma_tb=tb)

    def simulate(self, start):
        cnt = dict(start)
        pos = {e: 0 for e in self.ENG}
        total = sum(len(v) for v in self.q.values())
        done = 0
        while done < total:
            prog = False
            for e in self.ENG:
                while pos[e] < len(self.q[e]):
                    fn, w, sig, dma = self.q[e][pos[e]]
                    if any(cnt.get(s.name, 0) < v for (s, v) in w):
                        break
                    if sig is not None:
                        cnt[sig.name] = cnt.get(sig.name, 0) + (16 if dma else 1)
                    pos[e] += 1
                    done += 1
                    prog = True
            if not prog:
                print("DEADLOCK in phase", self.tag)
                for e in self.ENG:
                    if pos[e] < len(self.q[e]):
                        fn, w, sig, dma = self.q[e][pos[e]]
                        print(" ", e, pos[e], "/", len(self.q[e]), [(s.name, v, cnt.get(s.name, 0)) for (s, v) in w if cnt.get(s.name, 0) < v])
                raise RuntimeError("deadlock")
        for s in self.pool:
            assert cnt.get(s.name, 0) == s.n, (s.name, cnt.get(s.name, 0), s.n)
        print("phase", self.tag, "sim ok", {e: len(v) for e, v in self.q.items()})

    def finish(self):
        self.q["sp"].append((None, [(s, s.n) for s in self.dma_sems if s.n > 0], None, False))
        if os.environ.get("SIMCHECK"):
            self.simulate(self.start_counts)
        nc = self.nc
        q = self.q
        with nc.Block() as block:
            def run(e, items):
                seen = {}
                for fn, w, sig, dma in items:
                    for (s, v) in w:
                        if seen.get(s.name, -1) >= v:
                            continue
                        seen[s.name] = v
                        e.wait_ge(s.h, v)
                    if fn is None:
                        continue
                    inst = fn(e)
                    if sig is not None:
                        inst.then_inc(sig.h, 16 if dma else 1)

            @block.tensor
            def _(e):
                run(e, q["pe"])

            @block.scalar
            def _(e):
                run(e, q["act"])

            @block.vector
            def _(e):
                run(e, q["dve"])

            @block.gpsimd
            def _(e):
                run(e, q["pool"])

            @block.sync
            def _(e):
                run(e, q["sp"])


class WStream:
    def __init__(self, P, name, shape, nb, srcs):
        self.P = P
        self.nb = nb
        self.srcs = srcs
        self.bufs = [P.sb(f"{name}{i}", shape, BF16) for i in range(nb)]
        for i in range(min(nb, len(srcs))):
            self._issue(i)

    def _issue(self, i):
        b = self.bufs[i % self.nb]
        self.P.dma(b[:], self.srcs[i], writes=[b], tb=b)

    def get(self, i):
        return self.bufs[i % self.nb]

    def release(self, i):
        if i + self.nb < len(self.srcs):
            self._issue(i + self.nb)


def mm_group(P, out_ap, out_tb, pairs, reads):
    n = len(pairs)
    for k, (l, r) in enumerate(pairs):
        P.do("pe", lambda e, l=l, r=r, k=k: e.matmul(out_ap, l, r, start=(k == 0), stop=(k == n - 1)),
             reads=reads, writes=[out_tb], sig=(k == n - 1), check=(k == 0))


def phase_convert(nc, pool, W, Wt, K, F, tc, tag):
    c = 512 // tc
    NB = 4
    with ExitStack() as st:
        P = Prog(nc, st, pool, tag)
        stg = [P.sb(f"stg{i}", [128, 8, 512], F32) for i in range(NB)]
        wb = [P.sb(f"wb{i}", [128, c, 8, tc], BF16) for i in range(NB)]
        chunks = [(kg, cb) for cb in range(F // 512) for kg in range(K // 1024)]

        def load(i):
            kg, cb = chunks[i]
            b = stg[i % NB]
            P.dma(b[:], W[kg * 1024:(kg + 1) * 1024, cb * 512:(cb + 1) * 512].rearrange("(k p) f -> p k f", p=128),
                  writes=[b], tb=b)

        for i in range(min(NB - 1, len(chunks))):
            load(i)
        for i, (kg, cb) in enumerate(chunks):
            if i + NB - 1 < len(chunks):
                load(i + NB - 1)
            s = stg[i % NB]
            o = wb[i % NB]
            eng = CAST_ENGS[i % len(CAST_ENGS)]
            oap = o[:].rearrange("p c k t -> p k c t")
            iap = s[:].rearrange("p k (c t) -> p k c t", t=tc)
            if eng == "act":
                P.do("act", lambda e, oap=oap, iap=iap: e.activation(out=oap, in_=iap, func=AF.Copy), reads=[s], writes=[o])
            else:
                P.do(eng, lambda e, oap=oap, iap=iap: e.tensor_copy(out=oap, in_=iap), reads=[s], writes=[o])
            P.dma(Wt[cb * c:(cb + 1) * c, :, kg * 8:(kg + 1) * 8, :].rearrange("c p k t -> p c k t"), o[:], reads=[o], tb=o)
        P.finish()


def phase_a(nc, pool, G, io):
    with ExitStack() as st:
        P = Prog(nc, st, pool, "A")
        ident, rm, kmean = G["ident"], G["rm"], G["kmean"]
        xs = [P.sb(f"xs{i}", [128, D], F32) for i in range(2)]
        xb = [P.sb(f"xb{i}", [128, D], BF16) for i in range(2)]
        xT = P.sb("xT", [128, 32, 512], BF16)
        wpool = P.sb("wpool", [128, 16, 512], BF16)
        cs = [P.sb(f"cs{i}", [128, 512], F32) for i in range(2)]
        sn = [P.sb(f"sn{i}", [128, 512], F32) for i in range(2)]
        qraw = [P.sb(f"qraw{i}", [128, 512], BF16) for i in range(2)]
        qf = [P.sb(f"qf{i}", [128, 512], F32) for i in range(2)]
        t1 = [P.sb(f"t1{i}", [128, 512], F32) for i in range(2)]
        t2 = [P.sb(f"t2{i}", [128, 512], F32) for i in range(2)]
        qo = [P.sb(f"qo{i}", [128, 512], BF16) for i in range(2)]
        vtok = [P.sb(f"vtok{i}", [128, 4, 128], BF16) for i in range(2)]
        U = [P.sb(f"U{i}", [128, 528], F32) for i in range(2)]
        SA = [P.sb(f"SA{i}", [128, 528], F32) for i in range(2)]
        SB_ = [P.sb(f"SB{i}", [128, 528], F32) for i in range(2)]
        pooled = [P.sb(f"pooled{i}", [128, 4, 512], BF16) for i in range(2)]
        po = [P.sb(f"po{i}", [128, 512], BF16) for i in range(2)]
        t16 = P.sb("t16", [128, 16], F32)
        uh = P.sb("uh", [128, 16, 16], F32)
        invc = P.sb("invc", [128, 4, 16], F32)
        psc = P.sb("psc", [128, 16], F32)
        ptr = [P.ps(f"ptr{i}", [128, 512], BF16) for i in range(2)]
        pacc = [P.ps(f"pacc{i}", [128, 512], F32) for i in range(2)]
        prot = P.ps("prot", [128, 512], F32)
        ptv = P.ps("ptv", [128, 512], BF16)
        ppool = P.ps("ppool", [128, 512], F32)

        P.dma(wpool[:], io["wpool_t"][0], writes=[wpool], tb=wpool)
        P.dma(invc[:], io["invc"], writes=[invc], tb=invc)
        P.dma(psc[:], io["psc"], writes=[psc], tb=psc)

        srcs = []
        plan = []
        for gi in range(8):
            own = gi >= 4
            fts = []
            if own:
                fts += list(range(0, 16))
            fts += list(range(16, 48))
            if own or gi == 3:
                fts += list(range(48, 64))
            if os.environ.get("A_FTS"):
                keep = set(int(v) for v in os.environ["A_FTS"].split(","))
                fts = [f_ for f_ in fts if f_ in keep]
            if os.environ.get("A_GROUPS") and gi not in set(int(v) for v in os.environ["A_GROUPS"].split(",")):
                fts = []
            plan.append(fts)
            srcs += [io["win_t"][ft] for ft in fts]
        ws = WStream(P, "w", [128, 32, 128], 3, srcs)
        wi = 0
        ev = 0
        for gi in range(8):
            if not plan[gi]:
                continue
            own = gi >= 4
            xsrc = io["xo"] if own else io["xc"]
            tok0 = (gi % 4) * 512
            slot0 = gi * 512
            c_ = cs[gi % 2]
            s_ = sn[gi % 2]
            P.dma(c_[:], io["cosT"][:, slot0:slot0 + 512], writes=[c_], tb=c_)
            P.dma(s_[:], io["sinT"][:, slot0:slot0 + 512], writes=[s_], tb=s_)

            def xload(tt):
                b = xs[tt % 2]
                P.dma(b[:], xsrc[tok0 + tt * 128: tok0 + (tt + 1) * 128, :], writes=[b], tb=b)

            xload(0)
            for tt in range(4):
                if tt + 1 < 4:
                    xload(tt + 1)
                a = xs[tt % 2]
                b = xb[tt % 2]
                if tt % 2 == 0:
                    P.do("act", lambda e, a=a, b=b: e.activation(out=b[:], in_=a[:], func=AF.Copy), reads=[a], writes=[b])
                else:
                    P.do("pool", lambda e, a=a, b=b: e.tensor_copy(out=b[:], in_=a[:]), reads=[a], writes=[b])
                for k4 in range(8 if A_STOP >= 2 else 0):
                    p_ = ptr[k4 % 2]
                    for j in range(4):
                        kc = k4 * 4 + j
                        P.do("pe", lambda e, p_=p_, b=b, kc=kc, j=j: e.transpose(p_[:, j * 128:(j + 1) * 128], b[:, kc * 128:(kc + 1) * 128], ident[:]),
                             reads=[b, ident], writes=[p_], sig=(j == 3), check=(j == 0))
                    P.do("dve", lambda e, p_=p_, k4=k4, tt=tt: e.tensor_copy(out=xT[:, k4 * 4:(k4 + 1) * 4, tt * 128:(tt + 1) * 128],
                                                                          in_=p_[:].rearrange("p (a t) -> p a t", t=128)),
                         reads=[p_], writes=[xT])
            for ft in (plan[gi] if A_STOP >= 3 else []):
                w = ws.get(wi)
                pa = pacc[ev % 2]
                mm_group(P, pa[:], pa, [(w[:, kc, :], xT[:, kc, :]) for kc in range(32)], [w, xT])
                ws.release(wi)
                wi += 1
                b2 = ev % 2
                if A_STOP < 4:
                    ev += 1
                    continue
                if ft < 32:
                    h = ft % 16
                    isk = ft >= 16
                    qr, a1, a2, o_ = qraw[b2], t1[b2], t2[b2], qo[b2]
                    qf_ = qf[b2]
                    P.do("act", lambda e, qf_=qf_, pa=pa: e.activation(out=qf_[:], in_=pa[:], func=AF.Copy), reads=[pa], writes=[qf_])
                    P.do("pool", lambda e, qr=qr, qf_=qf_: e.tensor_copy(out=qr[:], in_=qf_[:]), reads=[qf_], writes=[qr])
                    P.do("pe", lambda e, qr=qr: e.matmul(prot[:], rm[:], qr[:], start=True, stop=True), reads=[qr, rm], writes=[prot])
                    P.do("dve", lambda e, a1=a1, qf_=qf_, c_=c_: e.tensor_tensor(out=a1[:], in0=qf_[:], in1=c_[:], op=ALU.mult), reads=[qf_, c_], writes=[a1])
                    P.do("act", lambda e, a2=a2: e.activation(out=a2[:], in_=prot[:], func=AF.Copy), reads=[prot], writes=[a2])
                    P.do("dve", lambda e, a2=a2, s_=s_: e.tensor_tensor(out=a2[:], in0=a2[:], in1=s_[:], op=ALU.mult), reads=[s_], writes=[a2])
                    P.do("pool", lambda e, a1=a1, a2=a2: e.tensor_tensor(out=a1[:], in0=a1[:], in1=a2[:], op=ALU.add), reads=[a2], writes=[a1])
                    if isk:
                        P.do("dve", lambda e, a1=a1, h=h, gi=gi: e.tensor_reduce(out=kmean[:, h, gi * 2:gi * 2 + 2],
                                                                                in_=a1[:].rearrange("p (b t) -> p b t", t=256),
                                                                                axis=AX.X, op=ALU.add), reads=[a1], writes=[kmean])
                    P.do("act", lambda e, a1=a1, o_=o_: e.activation(out=o_[:], in_=a1[:], func=AF.Copy), reads=[a1], writes=[o_])
                    if isk:
                        P.dma(io["kT"][h][:, slot0:slot0 + 512], o_[:], reads=[o_], tb=o_)
                    else:
                        P.dma(io["qT"][h][:, tok0:tok0 + 512], o_[:], reads=[o_], tb=o_)
                elif ft < 48:
                    h = ft - 32
                    qr, vt = qraw[b2], vtok[b2]
                    P.do("act", lambda e, qr=qr, pa=pa: e.activation(out=qr[:], in_=pa[:], func=AF.Copy), reads=[pa], writes=[qr])
                    for j in range(4):
                        P.do("pe", lambda e, qr=qr, j=j: e.transpose(ptv[:, j * 128:(j + 1) * 128], qr[:, j * 128:(j + 1) * 128], ident[:]),
                             reads=[qr, ident], writes=[ptv], sig=(j == 3), check=(j == 0))
                    P.do("dve", lambda e, vt=vt: e.tensor_copy(out=vt[:], in_=ptv[:].rearrange("p (a t) -> p a t", t=128)), reads=[ptv], writes=[vt])
                    P.dma(io["V"][h][:, gi * 4:(gi + 1) * 4, :], vt[:], reads=[vt], tb=vt)
                else:
                    cti = ft - 48
                    pg, ct = cti // 4, cti % 4
                    wdw = 2 << pg
                    u_, sa, sb2 = U[b2], SA[b2], SB_[b2]
                    pl = pooled[pg % 2]
                    P.do("act", lambda e, u_=u_, pa=pa: e.activation(out=u_[:, 16:528], in_=pa[:], func=AF.Copy), reads=[pa], writes=[u_])
                    if own:
                        P.do("pool", lambda e, u_=u_, cti=cti: e.tensor_copy(out=u_[:, 0:16], in_=uh[:, cti, :]), reads=[uh, u_], writes=[u_])
                    P.do("pool", lambda e, u_=u_, cti=cti: e.tensor_copy(out=uh[:, cti, :], in_=u_[:, 512:528]), reads=[u_], writes=[uh])
                    if own:
                        src, dst, oth = u_, sa, sb2
                        sh = 1
                        lo = 1
                        while sh < wdw:
                            eng = "dve" if (sh in (1, 4)) else "pool"
                            P.do(eng, lambda e, src=src, dst=dst, sh=sh, lo=lo: e.tensor_tensor(out=dst[:, lo:528], in0=src[:, lo:528], in1=src[:, lo - sh:528 - sh], op=ALU.add),
                                 reads=[src], writes=[dst])
                            src = dst
                            dst = oth if dst is sa else sa
                            oth = sa if dst is sb2 else sb2
                            sh *= 2
                            lo = 2 * sh - 1
                        sw = src
                        P.do("dve", lambda e, sw=sw, u_=u_, pl=pl, ct=ct, wdw=wdw: e.scalar_tensor_tensor(out=pl[:, ct, :], in0=sw[:, 16:528], scalar=1.0 / wdw, in1=u_[:, 16:528],
                                                                                                    op0=ALU.mult, op1=ALU.subtract), reads=[sw, u_], writes=[pl])
                        if gi == 4:
                            P.do("dve", lambda e, sw=sw, pg=pg: e.tensor_tensor(out=t16[:], in0=sw[:, 16:32], in1=invc[:, pg, :], op=ALU.mult), reads=[sw, invc], writes=[t16])
                            P.do("dve", lambda e, u_=u_, pl=pl, ct=ct: e.tensor_tensor(out=pl[:, ct, 0:16], in0=t16[:], in1=u_[:, 16:32], op=ALU.subtract), reads=[t16, u_], writes=[pl])
                        if ct == 3:
                            for et in range(4):
                                mm_group(P, ppool[:], ppool, [(wpool[:, pg * 4 + c2, et * 128:(et + 1) * 128], pl[:, c2, :]) for c2 in range(4)], [wpool, pl])
                                o_ = po[et % 2]
                                P.do("act", lambda e, o_=o_, pg=pg, et=et: e.activation(out=o_[:], in_=ppool[:], func=AF.Copy, scale=psc[:, pg * 4 + et:pg * 4 + et + 1]),
                                     reads=[ppool, psc], writes=[o_])
                                P.dma(io["mixT"][16 + pg * 4 + et][:, tok0:tok0 + 512], o_[:], reads=[o_], tb=o_)
                ev += 1
        P.finish()


def phase_b(nc, pool, G, io):
    with ExitStack() as st:
        P = Prog(nc, st, pool, "B")
        ident, kmean = G["ident"], G["kmean"]
        qT = [P.sb(f"qT{i}", [128, HALF], BF16) for i in range(2)]
        kT = [P.sb(f"kT{i}", [128, S], BF16) for i in range(2)]
        Vb = [P.sb(f"Vb{i}", [128, 32, 132], BF16) for i in range(2)]
        aT = [P.sb(f"aT{i}", [128, HALF], BF16) for i in range(2)]
        kmb = P.sb("kmb", [128, 16], BF16)
        gbias = P.sb("gbias", [128, 8, 2, 16], F32)
        tri = P.sb("tri", [128, 128], BF16)
        gm = P.sb("gm", [128, 2, 16], F32)
        top8 = P.sb("top8", [128, 2, 8], F32)
        thr = P.sb("thr", [128, 2], F32)
        sel = [P.sb(f"sel{i}", [128, 2, 16], F32) for i in range(2)]
        pTs = [P.sb(f"pTs{i}", [128, 512], BF16) for i in range(3)]
        acc = [P.sb(f"acc{i}", [128, 2, 132], F32) for i in range(2)]
        rec = P.sb("rec", [128, 2], F32)
        osb = [P.sb(f"osb{i}", [128, 2, 132], F32) for i in range(2)]
        ab = [P.sb(f"ab{i}", [128, 2, 128], BF16) for i in range(2)]
        pS = [P.ps(f"pS{i}", [128, 512], F32) for i in range(2)]
        pO = [P.ps(f"pO{i}", [128, 2, 256], F32) for i in range(2)]
        pG = P.ps("pG", [128, 2, 16], F32)
        pT = P.ps("pT", [128, 512], BF16)

        P.dma(gbias[:], io["gbias"], writes=[gbias], tb=gbias)
        P.dma(tri[:], io["tri"], writes=[tri], tb=tri)
        for i in range(2):
            P.do("pool", lambda e, i=i: e.memset(Vb[i][:, :, 128:129], 1.0), writes=[Vb[i]])

        def hload(h):
            b = h % 2
            P.dma(qT[b][:], io["qT"][h], writes=[qT[b]], tb=qT[b])
            P.dma(kT[b][:], io["kT"][h], writes=[kT[b]], tb=kT[b])
            P.dma(Vb[b][:, :, 0:128], io["V"][h], writes=[Vb[b]], tb=Vb[b])

        hload(0)
        un = 0
        for h in range(16):
            if h + 1 < 16:
                hload(h + 1)
            q_, k_, v_, a_ = qT[h % 2], kT[h % 2], Vb[h % 2], aT[h % 2]
            P.do("act", lambda e, h=h: e.activation(out=kmb[:], in_=kmean[:, h, :], func=AF.Copy, scale=1.0 / 256.0), reads=[kmean], writes=[kmb])
            for qb in range(8):
                ac = acc[qb % 2]
                sl = sel[qb % 2]
                q0 = qb * 256
                for a in range(2):
                    P.do("pe", lambda e, a=a, q_=q_, q0=q0: e.matmul(pG[:, a, :], q_[:, q0 + a * 128:q0 + (a + 1) * 128], kmb[:], start=True, stop=True),
                         reads=[q_, kmb], writes=[pG], sig=(a == 1), check=(a == 0))
                P.do("dve", lambda e: e.tensor_copy(out=gm[:], in_=pG[:]), reads=[pG], writes=[gm])
                P.do("dve", lambda e, qb=qb: e.tensor_tensor(out=gm[:], in0=gm[:], in1=gbias[:, qb, :, :], op=ALU.add), reads=[gbias], writes=[gm])
                for a in range(2):
                    P.do("dve", lambda e, a=a: e.max(out=top8[:, a, :], in_=gm[:, a, :]), reads=[gm], writes=[top8])
                P.do("dve", lambda e: e.tensor_scalar(out=thr[:], in0=top8[:, :, 2], scalar1=-1.0e29, scalar2=None, op0=ALU.max), reads=[top8], writes=[thr])
                for a in range(2):
                    P.do("dve", lambda e, a=a, sl=sl: e.tensor_scalar(out=sl[:, a, :], in0=gm[:, a, :], scalar1=thr[:, a:a + 1], scalar2=None, op0=ALU.is_ge),
                         reads=[gm, thr], writes=[sl])
                so = 8 + qb
                for s in [so] + list(range(0, so)):
                    ps_ = pS[un % 2]
                    po_ = pO[un % 2]
                    pt = pTs[un % 3]
                    k0 = s * 256
                    if s == so:
                        P.do("pe", lambda e, ps_=ps_, k_=k_, q_=q_, k0=k0, q0=q0: e.matmul(ps_[:, 0:256], k_[:, k0:k0 + 128], q_[:, q0:q0 + 256], start=True, stop=True),
                             reads=[k_, q_], writes=[ps_], sig=False)
                        P.do("pe", lambda e, ps_=ps_, k_=k_, q_=q_, k0=k0, q0=q0: e.matmul(ps_[:, 256:384], k_[:, k0 + 128:k0 + 256], q_[:, q0 + 128:q0 + 256], start=True, stop=True),
                             reads=[k_, q_], writes=[ps_], check=False)
                        P.do("act", lambda e, ps_=ps_, pt=pt: e.activation(out=pt[:, 0:384], in_=ps_[:, 0:384], func=AF.Exp, scale=SCALE), reads=[ps_], writes=[pt])
                        P.do("pool", lambda e, pt=pt: e.tensor_tensor(out=pt[:, 0:128], in0=pt[:, 0:128], in1=tri[:], op=ALU.mult), reads=[tri], writes=[pt])
                        P.do("pool", lambda e, pt=pt: e.tensor_tensor(out=pt[:, 256:384], in0=pt[:, 256:384], in1=tri[:], op=ALU.mult), reads=[tri], writes=[pt])
                        P.do("pe", lambda e, po_=po_, pt=pt, v_=v_, s=s: e.matmul(po_[:, 0, 0:129], pt[:, 0:128], v_[:, s * 2, 0:129], start=True, stop=True),
                             reads=[pt, v_], writes=[po_], sig=False)
                        P.do("pe", lambda e, po_=po_, pt=pt, v_=v_, s=s: e.matmul(po_[:, 1, 0:129], pt[:, 128:256], v_[:, s * 2, 0:129], start=True, stop=False),
                             reads=[pt, v_], writes=[po_], sig=False, check=False)
                        P.do("pe", lambda e, po_=po_, pt=pt, v_=v_, s=s: e.matmul(po_[:, 1, 0:129], pt[:, 256:384], v_[:, s * 2 + 1, 0:129], start=False, stop=True),
                             reads=[pt, v_], writes=[po_], check=False)
                        P.do("dve", lambda e, ac=ac, po_=po_: e.tensor_copy(out=ac[:, :, 0:129], in_=po_[:, :, 0:129]), reads=[po_], writes=[ac])
                    else:
                        for kt in range(2):
                            P.do("pe", lambda e, ps_=ps_, k_=k_, q_=q_, k0=k0, q0=q0, kt=kt: e.matmul(ps_[:, kt * 256:(kt + 1) * 256], k_[:, k0 + kt * 128:k0 + (kt + 1) * 128],
                                                                                                  q_[:, q0:q0 + 256], start=True, stop=True),
                                 reads=[k_, q_], writes=[ps_], sig=(kt == 1), check=(kt == 0))
                        P.do("act", lambda e, ps_=ps_, pt=pt: e.activation(out=pt[:], in_=ps_[:], func=AF.Exp, scale=SCALE), reads=[ps_], writes=[pt])
                        for a in range(2):
                            for kt in range(2):
                                P.do("pe", lambda e, po_=po_, pt=pt, v_=v_, s=s, a=a, kt=kt: e.matmul(po_[:, a, 0:129], pt[:, kt * 256 + a * 128:kt * 256 + (a + 1) * 128],
                                                                                              v_[:, s * 2 + kt, 0:129], start=(kt == 0), stop=(kt == 1)),
                                     reads=[pt, v_], writes=[po_], sig=(a == 1 and kt == 1), check=(a == 0 and kt == 0))
                        ob = osb[un % 2]
                        for a in range(2):
                            P.do("act", lambda e, ob=ob, po_=po_, sl=sl, a=a, s=s: e.activation(out=ob[:, a, 0:129], in_=po_[:, a, 0:129], func=AF.Copy, scale=sl[:, a, s:s + 1]),
                                 reads=[po_, sl], writes=[ob])
                        P.do("dve", lambda e, ac=ac, ob=ob: e.tensor_tensor(out=ac[:, :, 0:129], in0=ac[:, :, 0:129], in1=ob[:, :, 0:129], op=ALU.add),
                             reads=[ob], writes=[ac])
                    un += 1
                ab_ = ab[qb % 2]
                P.do("dve", lambda e, ac=ac: e.reciprocal(out=rec[:], in_=ac[:, :, 128]), reads=[ac], writes=[rec])
                for a in range(2):
                    P.do("dve", lambda e, ac=ac, ab_=ab_, a=a: e.tensor_scalar(out=ab_[:, a, :], in0=ac[:, a, 0:128], scalar1=rec[:, a:a + 1], scalar2=None, op0=ALU.mult),
                         reads=[ac, rec], writes=[ab_])
                for a in range(2):
                    c0 = (qb % 2) * 256 + a * 128
                    P.do("pe", lambda e, ab_=ab_, a=a, c0=c0: e.transpose(pT[:, c0:c0 + 128], ab_[:, a, :], ident[:]),
                         reads=[ab_, ident], writes=[pT], sig=(a == 1), check=(a == 0))
                if qb % 2 == 1:
                    P.do("act", lambda e, a_=a_, qb=qb: e.activation(out=a_[:, (qb - 1) * 256:(qb + 1) * 256], in_=pT[:], func=AF.Copy), reads=[pT], writes=[a_])
            P.dma(io["mixT"][h], a_[:], reads=[a_], tb=a_)
        P.finish()


def phase_de(nc, pool, G, io):
    TT = 512
    NT = TT // 128
    NG = HALF // TT
    with ExitStack() as st:
        P = Prog(nc, st, pool, "E")
        ident = G["ident"]
        hid = P.sb("hid", [128, 32, TT], BF16)
        y = P.sb("y", [128, NT, D], F32)
        hT = P.sb("hT", [128, 32, TT], BF16)
        xch = [P.sb(f"xch{i}", [128, 256], F32) for i in range(3)]
        gch = [P.sb(f"gch{i}", [128, 512], F32) for i in range(2)]
        bch = [P.sb(f"bch{i}", [128, 512], F32) for i in range(2)]
        rl = [P.sb(f"rl{i}", [128, TT], F32) for i in range(2)]
        evt = [P.sb(f"evt{i}", [128, 256], F32) for i in range(2)]
        hb = [P.sb(f"hb{i}", [128, 512], BF16) for i in range(2)]
        stats = P.sb("stats", [128, 8, 6], F32)
        mv = P.sb("mv", [128, 2], F32)
        sd = P.sb("sd", [128, 1], F32)
        rstd = P.sb("rstd", [128, 1], F32)
        nmr = P.sb("nmr", [128, 1], F32)
        epsb = P.sb("epsb", [128, 1], F32)
        pacc = [P.ps(f"pacc{i}", [128, 512], F32) for i in range(2)]
        pac2 = [P.ps(f"pac2{i}", [128, 512], F32) for i in range(4)]
        pT = P.ps("pT", [128, 512], BF16)
        P.do("pool", lambda e: e.memset(epsb[:], EPS), writes=[epsb])

        msrc = []
        w1src = []
        for tg in range(NG):
            msrc += [io["wout_t"][cb] for cb in range(16)]
            msrc += [io["wff2_t"][cb][:, kg * 32:(kg + 1) * 32, :] for kg in range(4) for cb in range(16)]
            w1src += [io["wff1_t"][ft] for ft in range(128)]
        wm = WStream(P, "wm", [128, 32, 256], 2, msrc)
        w1 = WStream(P, "w1", [128, 32, 128], 3, w1src)
        cnt = {"mi": 0, "fi": 0, "ev": 0, "ci": 0, "p2": 0, "xi": 0}

        def layer_norm(tt, gsrc, bsrc, tok0, final):
            for c in range(8):
                P.do("dve", lambda e, c=c: e.bn_stats(out=stats[:, c, :], in_=y[:, tt, c * 512:(c + 1) * 512]), reads=[y], writes=[stats])
            P.do("dve", lambda e: e.bn_aggr(out=mv[:], in_=stats[:].rearrange("p a b -> p (a b)")), reads=[stats], writes=[mv])
            P.do("act", lambda e: e.activation(out=sd[:], in_=mv[:, 1:2], func=AF.Sqrt, bias=epsb[:], scale=1.0), reads=[mv, epsb], writes=[sd])
            P.do("dve", lambda e: e.reciprocal(out=rstd[:], in_=sd[:]), reads=[sd], writes=[rstd])
            P.do("dve", lambda e: e.scalar_tensor_tensor(out=nmr[:], in0=mv[:, 0:1], scalar=-1.0, in1=rstd[:], op0=ALU.mult, op1=ALU.mult), reads=[mv, rstd], writes=[nmr])
            for c in range(8):
                g_, b_ = gch[cnt["ci"] % 2], bch[cnt["ci"] % 2]
                cnt["ci"] += 1
                P.dma(g_[:], gsrc[:, c * 512:(c + 1) * 512].partition_broadcast(128), writes=[g_], tb=g_)
                P.dma(b_[:], bsrc[:, c * 512:(c + 1) * 512].partition_broadcast(128), writes=[b_], tb=b_)
                ysl = y[:, tt, c * 512:(c + 1) * 512]
                P.do("dve", lambda e, ysl=ysl: e.tensor_scalar(out=ysl, in0=ysl, scalar1=rstd[:, 0:1], scalar2=nmr[:, 0:1], op0=ALU.mult, op1=ALU.add), reads=[rstd, nmr], writes=[y])
                P.do("pool", lambda e, ysl=ysl, g_=g_: e.tensor_tensor(out=ysl, in0=ysl, in1=g_[:], op=ALU.mult), reads=[g_], writes=[y])
                P.do("pool", lambda e, ysl=ysl, b_=b_: e.tensor_tensor(out=ysl, in0=ysl, in1=b_[:], op=ALU.add), reads=[b_], writes=[y])
                if not final:
                    hb_ = hb[c % 2]
                    P.do("act", lambda e, ysl=ysl, hb_=hb_: e.activation(out=hb_[:], in_=ysl, func=AF.Copy), reads=[y], writes=[hb_])
                    for j in range(4):
                        P.do("pe", lambda e, hb_=hb_, j=j: e.transpose(pT[:, j * 128:(j + 1) * 128], hb_[:, j * 128:(j + 1) * 128], ident[:]),
                             reads=[hb_, ident], writes=[pT], sig=(j == 3), check=(j == 0))
                    P.do("dve", lambda e, c=c: e.tensor_copy(out=hT[:, c * 4:(c + 1) * 4, tt * 128:(tt + 1) * 128], in_=pT[:].rearrange("p (a t) -> p a t", t=128)),
                         reads=[pT], writes=[hT])
            if final:
                P.dma(io["out"][tok0 + tt * 128:tok0 + (tt + 1) * 128, :], y[:, tt, :], reads=[y], tb=y)

        for tg in range(NG):
            tok0 = tg * TT
            P.dma(hid[:], io["mixT"][:, :, tok0:tok0 + TT].rearrange("k p t -> p k t"), writes=[hid], tb=hid)
            for cb in range(16):
                w = wm.get(cnt["mi"])
                for tt in range(NT):
                    xc_ = xch[cnt["xi"] % 3]
                    cnt["xi"] += 1
                    P.dma(xc_[:], io["xo"][tok0 + tt * 128:tok0 + (tt + 1) * 128, cb * 256:(cb + 1) * 256], writes=[xc_], tb=xc_)
                    pa = pac2[cnt["p2"] % 4]
                    cnt["p2"] += 1
                    mm_group(P, pa[:, 0:256], pa, [(hid[:, kc, tt * 128:(tt + 1) * 128], w[:, kc, :]) for kc in range(32)], [hid, w])
                    et_ = evt[cnt["ev"] % 2]
                    cnt["ev"] += 1
                    P.do("act", lambda e, et_=et_, pa=pa: e.activation(out=et_[:], in_=pa[:, 0:256], func=AF.Copy), reads=[pa], writes=[et_])
                    P.do("dve", lambda e, tt=tt, cb=cb, xc_=xc_, et_=et_: e.scalar_tensor_tensor(out=y[:, tt, cb * 256:(cb + 1) * 256], in0=xc_[:], scalar=ALPHA, in1=et_[:],
                                                                                            op0=ALU.mult, op1=ALU.add),
                         reads=[xc_, et_], writes=[y])
                wm.release(cnt["mi"])
                cnt["mi"] += 1
            for tt in range(NT):
                layer_norm(tt, io["ln1_g"], io["ln1_b"], tok0, False)
            for g in range(4):
                for f in range(32):
                    w = w1.get(cnt["fi"])
                    pa = pacc[cnt["fi"] % 2]
                    r_ = rl[cnt["fi"] % 2]
                    mm_group(P, pa[:], pa, [(w[:, kc, :], hT[:, kc, :]) for kc in range(32)], [w, hT])
                    w1.release(cnt["fi"])
                    cnt["fi"] += 1
                    P.do("act", lambda e, r_=r_, pa=pa: e.activation(out=r_[:], in_=pa[:], func=AF.Relu), reads=[pa], writes=[r_])
                    P.do("pool", lambda e, r_=r_, f=f: e.tensor_tensor(out=hid[:, f, :], in0=r_[:], in1=r_[:], op=ALU.mult), reads=[r_], writes=[hid])
                for cb in range(16):
                    w = wm.get(cnt["mi"])
                    for tt in range(NT):
                        pa = pac2[cnt["p2"] % 4]
                        cnt["p2"] += 1
                        mm_group(P, pa[:, 0:256], pa, [(hid[:, k, tt * 128:(tt + 1) * 128], w[:, k, :]) for k in range(32)], [hid, w])
                        et_ = evt[cnt["ev"] % 2]
                        cnt["ev"] += 1
                        ysl = y[:, tt, cb * 256:(cb + 1) * 256]
                        P.do("act", lambda e, et_=et_, pa=pa: e.activation(out=et_[:], in_=pa[:, 0:256], func=AF.Copy), reads=[pa], writes=[et_])
                        P.do("dve", lambda e, ysl=ysl, et_=et_, g=g: e.scalar_tensor_tensor(out=ysl, in0=ysl, scalar=(ALPHA if g == 0 else 1.0), in1=et_[:], op0=ALU.mult, op1=ALU.add),
                             reads=[et_], writes=[y])
                    wm.release(cnt["mi"])
                    cnt["mi"] += 1
            for tt in range(NT):
                layer_norm(tt, io["ln2_g"], io["ln2_b"], tok0, True)
        P.finish()


def build(phases=None, dbg=False):
    nc = bass.Bass("TRN2", target_bir_lowering=False)

    def din(name, shape, dt=F32):
        return nc.dram_tensor(name, shape, dt, kind="ExternalInput").ap()

    def scr(name, shape, dt=BF16):
        kind = "ExternalOutput" if (dbg and name in ("qT", "kT", "V", "mixT")) else "Internal"
        return nc.dram_tensor(name, shape, dt, kind=kind).ap()

    io = {}
    io["xo"] = din("xo", [HALF, D])
    io["xc"] = din("xc", [HALF, D])
    w_in = din("w_in", [D, 8192])
    w_pool = din("w_pool", [2048, 512])
    w_out = din("w_out", [D, D])
    w_ff1 = din("w_ff1", [D, DFF])
    w_ff2 = din("w_ff2", [DFF, D])
    for n in ("ln1_g", "ln1_b", "ln2_g", "ln2_b"):
        io[n] = din(n, [1, D])
    io["cosT"] = din("cosT", [128, S])
    io["sinT"] = din("sinT", [128, S])
    io["psc"] = din("psc", [128, 16])
    io["invc"] = din("invc", [128, 4, 16])
    io["gbias"] = din("gbias", [128, 8, 2, 16])
    c_ident = din("ident", [128, 128], BF16)
    c_rm = din("rm", [128, 128], BF16)
    io["tri"] = din("tri", [128, 128], BF16)
    io["out"] = nc.dram_tensor("out", [HALF, D], F32, kind="ExternalOutput").ap()
    io["win_t"] = scr("win_t", [64, 128, 32, 128])
    io["wff1_t"] = scr("wff1_t", [128, 128, 32, 128])
    io["wout_t"] = scr("wout_t", [16, 128, 32, 256])
    io["wff2_t"] = scr("wff2_t", [16, 128, 128, 256])
    io["wpool_t"] = scr("wpool_t", [1, 128, 16, 512])
    io["qT"] = scr("qT", [16, 128, HALF])
    io["kT"] = scr("kT", [16, 128, S])
    io["V"] = scr("V", [16, 128, 32, 128])
    io["mixT"] = scr("mixT", [32, 128, HALF])

    with ExitStack() as gst:
        pool = [Sem(gst.enter_context(nc.semaphore(f"gs{i}")), f"gs{i}") for i in range(96)]
        G = {}
        G["ident"] = TB(gst.enter_context(nc.sbuf_tensor("g_ident", [128, 128], BF16)), const=True)
        G["rm"] = TB(gst.enter_context(nc.sbuf_tensor("g_rm", [128, 128], BF16)), const=True)
        G["kmean"] = TB(gst.enter_context(nc.sbuf_tensor("g_kmean", [128, 16, 16], F32)))
        with ExitStack() as st:
            P = Prog(nc, st, pool, "I")
            P.dma(G["ident"][:], c_ident, tb=G["ident"])
            P.dma(G["rm"][:], c_rm, tb=G["rm"])
            P.finish()
        ph = phases or ("W0", "W1", "A", "W2", "W3", "W4", "B", "E")
        if "W0" in ph:
            phase_convert(nc, pool, w_in, io["win_t"], D, 8192, 128, "W0")
        if "W1" in ph:
            phase_convert(nc, pool, w_pool, io["wpool_t"], 2048, 512, 512, "W1")
        if "A" in ph:
            phase_a(nc, pool, G, io)
        if "W2" in ph:
            phase_convert(nc, pool, w_out, io["wout_t"], D, D, 256, "W2")
        if "W3" in ph:
            phase_convert(nc, pool, w_ff1, io["wff1_t"], D, DFF, 128, "W3")
        if "W4" in ph:
            phase_convert(nc, pool, w_ff2, io["wff2_t"], DFF, D, 256, "W4")
        if "B" in ph:
            phase_b(nc, pool, G, io)
        if "E" in ph:
            phase_de(nc, pool, G, io)
    return nc


_NC = None
_DEBUG_RET_MAPS = False


def _consts():
    bf = ml_dtypes.bfloat16
    ident = np.eye(128, dtype=np.float32).astype(bf)
    rm = np.zeros((128, 128), np.float32)
    for m in range(64):
        rm[m + 64, m] = -1.0
        rm[m, m + 64] = 1.0
    rm = rm.astype(bf)
    kk = np.arange(128)[:, None]
    qq = np.arange(128)[None, :]
    tri = (kk <= qq).astype(np.float32).astype(bf)
    inv_freq = (1.0 / (np.float32(10000.0) ** (np.arange(0, 128, 2, dtype=np.float32) / np.float32(128)))).astype(np.float32)
    return ident, rm, tri, inv_freq


def kernel(x, w_in, w_pool, pool_scale, w_out, ln1_g, ln1_b, w_ff1, w_ff2, ln2_g, ln2_b):
    global _NC
    if _NC is None:
        _NC = build()
    nc = _NC
    ident, rm, tri, inv_freq = _consts()
    x = np.asarray(x, np.float32)
    f = lambda a: np.ascontiguousarray(np.asarray(a, np.float32))
    shared = {
        "w_in": f(w_in[0]), "w_pool": f(np.asarray(w_pool[0]).reshape(2048, 512)), "w_out": f(w_out[0]),
        "w_ff1": f(w_ff1[0]), "w_ff2": f(w_ff2[0]),
        "ln1_g": f(ln1_g), "ln1_b": f(ln1_b), "ln2_g": f(ln2_g), "ln2_b": f(ln2_b),
        "psc": f(np.asarray(pool_scale[0]).reshape(16, 128).T),
        "ident": ident, "rm": rm, "tri": tri,
    }
    in_maps = []
    for c in range(8):
        b, half = c // 2, c % 2
        xo = f(x[b, half * HALF:(half + 1) * HALF])
        xc = f(x[b, 0:HALF]) if half == 1 else np.zeros((HALF, D), np.float32)
        pos = np.concatenate([np.arange(HALF), half * HALF + np.arange(HALF)]).astype(np.float32)
        ang = pos[:, None] * inv_freq[None, :]
        cosv = np.cos(ang).astype(np.float32)
        sinv = np.sin(ang).astype(np.float32)
        cosT = np.ascontiguousarray(np.concatenate([cosv, cosv], axis=1).T)
        sinT = np.ascontiguousarray(np.concatenate([sinv, sinv], axis=1).T)
        gb = np.full((8, 16), NEG, np.float32)
        for i in range(8):
            for s in range(16):
                if s < 8 + i and (half == 1 or s >= 8):
                    gb[i, s] = 0.0
        gbias = np.ascontiguousarray(np.broadcast_to(gb[None, :, None, :], (128, 8, 2, 16))).astype(np.float32)
        invc = np.zeros((4, 16), np.float32)
        for g, w in enumerate((2, 4, 8, 16)):
            for t in range(16):
                invc[g, t] = 1.0 / (min(t + 1, w) if half == 0 else w)
        invc = np.ascontiguousarray(np.broadcast_to(invc[None], (128, 4, 16))).astype(np.float32)
        m = dict(shared)
        m.update({"xo": xo, "xc": xc, "cosT": cosT, "sinT": sinT, "gbias": gbias, "invc": invc})
        in_maps.append(m)
    if _DEBUG_RET_MAPS:
        return in_maps
    res = run_bass_kernel_spmd(nc, in_maps, core_ids=list(range(8)))
    out = np.empty((NB_, S, D), np.float32)
    for c in range(8):
        b, half = c // 2, c % 2
        out[b, half * HALF:(half + 1) * HALF] = np.asarray(res.results[c]["out"], np.float32)
    return out
```

```python
import numpy as np
import ml_dtypes
from contextlib import ExitStack
import concourse.bass as bass
import concourse.mybir as mybir
from concourse.bass_utils import run_bass_kernel_spmd

F32 = mybir.dt.float32
BF16 = mybir.dt.bfloat16
AF = mybir.ActivationFunctionType
ALU = mybir.AluOpType
AX = mybir.AxisListType

D = 4096
S = 4096
NB_ = 4
HALF = 2048
DFF = 16384
ALPHA = float(2.0 ** 0.25)
EPS = 1e-5
SCALE = float(128 ** -0.5)
NEG = -1.0e30
import os
A_STOP = int(os.environ.get('A_STOP', '9'))
CAST_ENGS = tuple(os.environ.get('CAST_ENGS', 'act').split(','))


class Sem:
    def __init__(self, h, name):
        self.h = h
        self.n = 0
        self.name = name


class TB:
    def __init__(self, t, const=False):
        self.t = t
        self.w = {}
        self.r = {}
        self.const = const
        self.sem = None

    def __getitem__(self, idx):
        return self.t[idx]


class Prog:
    ENG = ("pe", "act", "dve", "pool", "sp")

    def __init__(self, nc, stack, pool, tag):
        self.nc = nc
        self.stack = stack
        self.pool = pool
        self.pi = 0
        self.tag = tag
        self.q = {e: [] for e in self.ENG}
        self.esem = {e: self.sem() for e in ("pe", "act", "dve", "pool")}
        self.dma_sems = []
        self.nt = 0
        self.start_counts = {s.name: s.n for s in pool}

    def sem(self):
        s = self.pool[self.pi]
        self.pi += 1
        return s

    def sb(self, name, shape, dt, const=False):
        self.nt += 1
        return TB(self.stack.enter_context(self.nc.sbuf_tensor(f"{self.tag}_{name}", shape, dt)), const)

    def ps(self, name, shape, dt):
        return TB(self.stack.enter_context(self.nc.psum_tensor(f"{self.tag}_{name}", shape, dt)))

    def do(self, eng, fn, reads=(), writes=(), sig=True, check=True, dma_tb=None):
        waits = {}

        def add(d):
            for k, (s, v) in d.items():
                if k not in waits or waits[k][1] < v:
                    waits[k] = (s, v)

        if check:
            for tb in reads:
                add(tb.w)
            for tb in writes:
                add(tb.w)
                add(tb.r)
        sem = None
        dma = dma_tb is not None
        if sig:
            if dma:
                if dma_tb.sem is None:
                    dma_tb.sem = self.sem()
                    self.dma_sems.append(dma_tb.sem)
                sem = dma_tb.sem
                sem.n += 16
            else:
                sem = self.esem[eng]
                sem.n += 1
            for tb in writes:
                tb.w[sem.name] = (sem, sem.n)
                tb.r = {}
            for tb in reads:
                if not tb.const:
                    tb.r[sem.name] = (sem, sem.n)
        self.q[eng].append((fn, list(waits.values()), sem, dma))

    def dma(self, out, in_, reads=(), writes=(), tb=None):
        self.do("sp", lambda e: e.dma_start(out=out, in_=in_), reads, writes, dma_tb=tb)

    def simulate(self, start):
        cnt = dict(start)
        pos = {e: 0 for e in self.ENG}
        total = sum(len(v) for v in self.q.values())
        done = 0
        while done < total:
            prog = False
            for e in self.ENG:
                while pos[e] < len(self.q[e]):
                    fn, w, sig, dma = self.q[e][pos[e]]
                    if any(cnt.get(s.name, 0) < v for (s, v) in w):
                        break
                    if sig is not None:
                        cnt[sig.name] = cnt.get(sig.name, 0) + (16 if dma else 1)
                    pos[e] += 1
                    done += 1
                    prog = True
            if not prog:
                print("DEADLOCK in phase", self.tag)
                for e in self.ENG:
                    if pos[e] < len(self.q[e]):
                        fn, w, sig, dma = self.q[e][pos[e]]
                        print(" ", e, pos[e], "/", len(self.q[e]), [(s.name, v, cnt.get(s.name, 0)) for (s, v) in w if cnt.get(s.name, 0) < v])
                raise RuntimeError("deadlock")
        for s in self.pool:
            assert cnt.get(s.name, 0) == s.n, (s.name, cnt.get(s.name, 0), s.n)
        print("phase", self.tag, "sim ok", {e: len(v) for e, v in self.q.items()})

    def finish(self):
        self.q["sp"].append((None, [(s, s.n) for s in self.dma_sems if s.n > 0], None, False))
        if os.environ.get("SIMCHECK"):
            self.simulate(self.start_counts)
        nc = self.nc
        q = self.q
        with nc.Block() as block:
            def run(e, items):
                seen = {}
                for fn, w, sig, dma in items:
                    for (s, v) in w:
                        if seen.get(s.name, -1) >= v:
                            continue
                        seen[s.name] = v
                        e.wait_ge(s.h, v)
                    if fn is None:
                        continue
                    inst = fn(e)
                    if sig is not None:
                        inst.then_inc(sig.h, 16 if dma else 1)

            @block.tensor
            def _(e):
                run(e, q["pe"])

            @block.scalar
            def _(e):
                run(e, q["act"])

            @block.vector
            def _(e):
                run(e, q["dve"])

            @block.gpsimd
            def _(e):
                run(e, q["pool"])

            @block.sync
            def _(e):
                run(e, q["sp"])


class WStream:
    def __init__(self, P, name, shape, nb, srcs):
        self.P = P
        self.nb = nb
        self.srcs = srcs
        self.bufs = [P.sb(f"{name}{i}", shape, BF16) for i in range(nb)]
        for i in range(min(nb, len(srcs))):
            self._issue(i)

    def _issue(self, i):
        b = self.bufs[i % self.nb]
        self.P.dma(b[:], self.srcs[i], writes=[b], tb=b)

    def get(self, i):
        return self.bufs[i % self.nb]

    def release(self, i):
        if i + self.nb < len(self.srcs):
            self._issue(i + self.nb)


def mm_group(P, out_ap, out_tb, pairs, reads):
    n = len(pairs)
    for k, (l, r) in enumerate(pairs):
        P.do("pe", lambda e, l=l, r=r, k=k: e.matmul(out_ap, l, r, start=(k == 0), stop=(k == n - 1)),
             reads=reads, writes=[out_tb], sig=(k == n - 1), check=(k == 0))


def phase_convert(nc, pool, W, Wt, K, F, tc, tag):
    c = 512 // tc
    NB = 4
    with ExitStack() as st:
        P = Prog(nc, st, pool, tag)
        stg = [P.sb(f"stg{i}", [128, 8, 512], F32) for i in range(NB)]
        wb = [P.sb(f"wb{i}", [128, c, 8, tc], BF16) for i in range(NB)]
        chunks = [(kg, cb) for cb in range(F // 512) for kg in range(K // 1024)]

        def load(i):
            kg, cb = chunks[i]
            b = stg[i % NB]
            P.dma(b[:], W[kg * 1024:(kg + 1) * 1024, cb * 512:(cb + 1) * 512].rearrange("(k p) f -> p k f", p=128),
                  writes=[b], tb=b)

        for i in range(min(NB - 1, len(chunks))):
            load(i)
        for i, (kg, cb) in enumerate(chunks):
            if i + NB - 1 < len(chunks):
                load(i + NB - 1)
            s = stg[i % NB]
            o = wb[i % NB]
            eng = CAST_ENGS[i % len(CAST_ENGS)]
            oap = o[:].rearrange("p c k t -> p k c t")
            iap = s[:].rearrange("p k (c t) -> p k c t", t=tc)
            if eng == "act":
                P.do("act", lambda e, oap=oap, iap=iap: e.activation(out=oap, in_=iap, func=AF.Copy), reads=[s], writes=[o])
            else:
                P.do(eng, lambda e, oap=oap, iap=iap: e.tensor_copy(out=oap, in_=iap), reads=[s], writes=[o])
            P.dma(Wt[cb * c:(cb + 1) * c, :, kg * 8:(kg + 1) * 8, :].rearrange("c p k t -> p c k t"), o[:], reads=[o], tb=o)
        P.finish()


def phase_a(nc, pool, G, io):
    with ExitStack() as st:
        P = Prog(nc, st, pool, "A")
        ident, rm, kmean = G["ident"], G["rm"], G["kmean"]
        xs = [P.sb(f"xs{i}", [128, D], F32) for i in range(2)]
        xb = [P.sb(f"xb{i}", [128, D], BF16) for i in range(2)]
        xT = P.sb("xT", [128, 32, 512], BF16)
        wpool = P.sb("wpool", [128, 16, 512], BF16)
        cs = [P.sb(f"cs{i}", [128, 512], F32) for i in range(2)]
        sn = [P.sb(f"sn{i}", [128, 512], F32) for i in range(2)]
        qraw = [P.sb(f"qraw{i}", [128, 512], BF16) for i in range(2)]
        qf = [P.sb(f"qf{i}", [128, 512], F32) for i in range(2)]
        t1 = [P.sb(f"t1{i}", [128, 512], F32) for i in range(2)]
        t2 = [P.sb(f"t2{i}", [128, 512], F32) for i in range(2)]
        qo = [P.sb(f"qo{i}", [128, 512], BF16) for i in range(2)]
        vtok = [P.sb(f"vtok{i}", [128, 4, 128], BF16) for i in range(2)]
        U = [P.sb(f"U{i}", [128, 528], F32) for i in range(2)]
        SA = [P.sb(f"SA{i}", [128, 528], F32) for i in range(2)]
        SB_ = [P.sb(f"SB{i}", [128, 528], F32) for i in range(2)]
        pooled = [P.sb(f"pooled{i}", [128, 4, 512], BF16) for i in range(2)]
        po = [P.sb(f"po{i}", [128, 512], BF16) for i in range(2)]
        t16 = P.sb("t16", [128, 16], F32)
        uh = P.sb("uh", [128, 16, 16], F32)
        invc = P.sb("invc", [128, 4, 16], F32)
        psc = P.sb("psc", [128, 16], F32)
        ptr = [P.ps(f"ptr{i}", [128, 512], BF16) for i in range(2)]
        pacc = [P.ps(f"pacc{i}", [128, 512], F32) for i in range(2)]
        prot = P.ps("prot", [128, 512], F32)
        ptv = P.ps("ptv", [128, 512], BF16)
        ppool = P.ps("ppool", [128, 512], F32)

        P.dma(wpool[:], io["wpool_t"][0], writes=[wpool], tb=wpool)
        P.dma(invc[:], io["invc"], writes=[invc], tb=invc)
        P.dma(psc[:], io["psc"], writes=[psc], tb=psc)

        srcs = []
        plan = []
        for gi in range(8):
            own = gi >= 4
            fts = []
            if own:
                fts += list(range(0, 16))
            fts += list(range(16, 48))
            if own or gi == 3:
                fts += list(range(48, 64))
            if os.environ.get("A_FTS"):
                keep = set(int(v) for v in os.environ["A_FTS"].split(","))
                fts = [f_ for f_ in fts if f_ in keep]
            if os.environ.get("A_GROUPS") and gi not in set(int(v) for v in os.environ["A_GROUPS"].split(",")):
                fts = []
            plan.append(fts)
            srcs += [io["win_t"][ft] for ft in fts]
        ws = WStream(P, "w", [128, 32, 128], 3, srcs)
        wi = 0
        ev = 0
        for gi in range(8):
            if not plan[gi]:
                continue
            own = gi >= 4
            xsrc = io["xo"] if own else io["xc"]
            tok0 = (gi % 4) * 512
            slot0 = gi * 512
            c_ = cs[gi % 2]
            s_ = sn[gi % 2]
            P.dma(c_[:], io["cosT"][:, slot0:slot0 + 512], writes=[c_], tb=c_)
            P.dma(s_[:], io["sinT"][:, slot0:slot0 + 512], writes=[s_], tb=s_)

            def xload(tt):
                b = xs[tt % 2]
                P.dma(b[:], xsrc[tok0 + tt * 128: tok0 + (tt + 1) * 128, :], writes=[b], tb=b)

            xload(0)
            for tt in range(4):
                if tt + 1 < 4:
                    xload(tt + 1)
                a = xs[tt % 2]
                b = xb[tt % 2]
                if tt % 2 == 0:
                    P.do("act", lambda e, a=a, b=b: e.activation(out=b[:], in_=a[:], func=AF.Copy), reads=[a], writes=[b])
                else:
                    P.do("pool", lambda e, a=a, b=b: e.tensor_copy(out=b[:], in_=a[:]), reads=[a], writes=[b])
                for k4 in range(8 if A_STOP >= 2 else 0):
                    p_ = ptr[k4 % 2]
                    for j in range(4):
                        kc = k4 * 4 + j
                        P.do("pe", lambda e, p_=p_, b=b, kc=kc, j=j: e.transpose(p_[:, j * 128:(j + 1) * 128], b[:, kc * 128:(kc + 1) * 128], ident[:]),
                             reads=[b, ident], writes=[p_], sig=(j == 3), check=(j == 0))
                    P.do("dve", lambda e, p_=p_, k4=k4, tt=tt: e.tensor_copy(out=xT[:, k4 * 4:(k4 + 1) * 4, tt * 128:(tt + 1) * 128],
                                                                          in_=p_[:].rearrange("p (a t) -> p a t", t=128)),
                         reads=[p_], writes=[xT])
            for ft in (plan[gi] if A_STOP >= 3 else []):
                w = ws.get(wi)
                pa = pacc[ev % 2]
                mm_group(P, pa[:], pa, [(w[:, kc, :], xT[:, kc, :]) for kc in range(32)], [w, xT])
                ws.release(wi)
                wi += 1
                b2 = ev % 2
                if A_STOP < 4:
                    ev += 1
                    continue
                if ft < 32:
                    h = ft % 16
                    isk = ft >= 16
                    qr, a1, a2, o_ = qraw[b2], t1[b2], t2[b2], qo[b2]
                    qf_ = qf[b2]
                    P.do("act", lambda e, qf_=qf_, pa=pa: e.activation(out=qf_[:], in_=pa[:], func=AF.Copy), reads=[pa], writes=[qf_])
                    P.do("pool", lambda e, qr=qr, qf_=qf_: e.tensor_copy(out=qr[:], in_=qf_[:]), reads=[qf_], writes=[qr])
                    P.do("pe", lambda e, qr=qr: e.matmul(prot[:], rm[:], qr[:], start=True, stop=True), reads=[qr, rm], writes=[prot])
                    P.do("dve", lambda e, a1=a1, qf_=qf_, c_=c_: e.tensor_tensor(out=a1[:], in0=qf_[:], in1=c_[:], op=ALU.mult), reads=[qf_, c_], writes=[a1])
                    P.do("act", lambda e, a2=a2: e.activation(out=a2[:], in_=prot[:], func=AF.Copy), reads=[prot], writes=[a2])
                    P.do("dve", lambda e, a2=a2, s_=s_: e.tensor_tensor(out=a2[:], in0=a2[:], in1=s_[:], op=ALU.mult), reads=[s_], writes=[a2])
                    P.do("pool", lambda e, a1=a1, a2=a2: e.tensor_tensor(out=a1[:], in0=a1[:], in1=a2[:], op=ALU.add), reads=[a2], writes=[a1])
                    if isk:
                        P.do("dve", lambda e, a1=a1, h=h, gi=gi: e.tensor_reduce(out=kmean[:, h, gi * 2:gi * 2 + 2],
                                                                                in_=a1[:].rearrange("p (b t) -> p b t", t=256),
                                                                                axis=AX.X, op=ALU.add), reads=[a1], writes=[kmean])
                    P.do("act", lambda e, a1=a1, o_=o_: e.activation(out=o_[:], in_=a1[:], func=AF.Copy), reads=[a1], writes=[o_])
                    if isk:
                        P.dma(io["kT"][h][:, slot0:slot0 + 512], o_[:], reads=[o_], tb=o_)
                    else:
                        P.dma(io["qT"][h][:, tok0:tok0 + 512], o_[:], reads=[o_], tb=o_)
                elif ft < 48:
                    h = ft - 32
                    qr, vt = qraw[b2], vtok[b2]
                    P.do("act", lambda e, qr=qr, pa=pa: e.activation(out=qr[:], in_=pa[:], func=AF.Copy), reads=[pa], writes=[qr])
                    for j in range(4):
                        P.do("pe", lambda e, qr=qr, j=j: e.transpose(ptv[:, j * 128:(j + 1) * 128], qr[:, j * 128:(j + 1) * 128], ident[:]),
                             reads=[qr, ident], writes=[ptv], sig=(j == 3), check=(j == 0))
                    P.do("dve", lambda e, vt=vt: e.tensor_copy(out=vt[:], in_=ptv[:].rearrange("p (a t) -> p a t", t=128)), reads=[ptv], writes=[vt])
                    P.dma(io["V"][h][:, gi * 4:(gi + 1) * 4, :], vt[:], reads=[vt], tb=vt)
                else:
                    cti = ft - 48
                    pg, ct = cti // 4, cti % 4
                    wdw = 2 << pg
                    u_, sa, sb2 = U[b2], SA[b2], SB_[b2]
                    pl = pooled[pg % 2]
                    P.do("act", lambda e, u_=u_, pa=pa: e.activation(out=u_[:, 16:528], in_=pa[:], func=AF.Copy), reads=[pa], writes=[u_])
                    if own:
                        P.do("pool", lambda e, u_=u_, cti=cti: e.tensor_copy(out=u_[:, 0:16], in_=uh[:, cti, :]), reads=[uh, u_], writes=[u_])
                    P.do("pool", lambda e, u_=u_, cti=cti: e.tensor_copy(out=uh[:, cti, :], in_=u_[:, 512:528]), reads=[u_], writes=[uh])
                    if own:
                        src, dst, oth = u_, sa, sb2
                        sh = 1
                        lo = 1
                        while sh < wdw:
                            eng = "dve" if (sh in (1, 4)) else "pool"
                            P.do(eng, lambda e, src=src, dst=dst, sh=sh, lo=lo: e.tensor_tensor(out=dst[:, lo:528], in0=src[:, lo:528], in1=src[:, lo - sh:528 - sh], op=ALU.add),
                                 reads=[src], writes=[dst])
                            src = dst
                            dst = oth if dst is sa else sa
                            oth = sa if dst is sb2 else sb2
                            sh *= 2
                            lo = 2 * sh - 1
                        sw = src
                        P.do("dve", lambda e, sw=sw, u_=u_, pl=pl, ct=ct, wdw=wdw: e.scalar_tensor_tensor(out=pl[:, ct, :], in0=sw[:, 16:528], scalar=1.0 / wdw, in1=u_[:, 16:528],
                                                                                                    op0=ALU.mult, op1=ALU.subtract), reads=[sw, u_], writes=[pl])
                        if gi == 4:
                            P.do("dve", lambda e, sw=sw, pg=pg: e.tensor_tensor(out=t16[:], in0=sw[:, 16:32], in1=invc[:, pg, :], op=ALU.mult), reads=[sw, invc], writes=[t16])
                            P.do("dve", lambda e, u_=u_, pl=pl, ct=ct: e.tensor_tensor(out=pl[:, ct, 0:16], in0=t16[:], in1=u_[:, 16:32], op=ALU.subtract), reads=[t16, u_], writes=[pl])
                        if ct == 3:
                            for et in range(4):
                                mm_group(P, ppool[:], ppool, [(wpool[:, pg * 4 + c2, et * 128:(et + 1) * 128], pl[:, c2, :]) for c2 in range(4)], [wpool, pl])
                                o_ = po[et % 2]
                                P.do("act", lambda e, o_=o_, pg=pg, et=et: e.activation(out=o_[:], in_=ppool[:], func=AF.Copy, scale=psc[:, pg * 4 + et:pg * 4 + et + 1]),
                                     reads=[ppool, psc], writes=[o_])
                                P.dma(io["mixT"][16 + pg * 4 + et][:, tok0:tok0 + 512], o_[:], reads=[o_], tb=o_)
                ev += 1
        P.finish()


def phase_b(nc, pool, G, io, conv_specs=()):
    with ExitStack() as st:
        P = Prog(nc, st, pool, "B")
        CNB = 4
        cchunks = []
        for (W, Wt, K, F, tc) in conv_specs:
            c = 512 // tc
            cchunks += [(W, Wt, kg, cb, c, tc) for cb in range(F // 512) for kg in range(K // 1024)]
        cstg = [P.sb(f"cstg{i}", [128, 8, 512], F32) for i in range(CNB)] if cchunks else []
        cwb = [P.sb(f"cwb{i}", [128, 4096], BF16) for i in range(CNB)] if cchunks else []
        cst = {"i": 0}

        def cload(i):
            W, Wt, kg, cb, c, tc = cchunks[i]
            b = cstg[i % CNB]
            P.dma(b[:], W[kg * 1024:(kg + 1) * 1024, cb * 512:(cb + 1) * 512].rearrange("(k p) f -> p k f", p=128), writes=[b], tb=b)

        def conv_step():
            i = cst["i"]
            if i >= len(cchunks):
                return
            if i + CNB - 1 < len(cchunks):
                cload(i + CNB - 1)
            W, Wt, kg, cb, c, tc = cchunks[i]
            s_ = cstg[i % CNB]
            o = cwb[i % CNB]
            oap = o[:].rearrange("p (c k t) -> p k c t", c=c, k=8)
            iap = s_[:].rearrange("p k (c t) -> p k c t", t=tc)
            if i % 2 == 0:
                P.do("act", lambda e, oap=oap, iap=iap: e.activation(out=oap, in_=iap, func=AF.Copy), reads=[s_], writes=[o])
            else:
                P.do("dve", lambda e, oap=oap, iap=iap: e.tensor_copy(out=oap, in_=iap), reads=[s_], writes=[o])
            P.dma(Wt[cb * c:(cb + 1) * c, :, kg * 8:(kg + 1) * 8, :].rearrange("c p k t -> p c k t"), o[:].rearrange("p (c k t) -> p c k t", c=c, k=8), reads=[o], tb=o)
            cst["i"] = i + 1

        for i in range(min(CNB - 1, len(cchunks))):
            cload(i)
        ident, kmean = G["ident"], G["kmean"]
        qT = [P.sb(f"qT{i}", [128, HALF], BF16) for i in range(2)]
        kT = [P.sb(f"kT{i}", [128, S], BF16) for i in range(2)]
        Vb = [P.sb(f"Vb{i}", [128, 32, 132], BF16) for i in range(2)]
        aT = [P.sb(f"aT{i}", [128, HALF], BF16) for i in range(2)]
        kmb = P.sb("kmb", [128, 16], BF16)
        gbias = P.sb("gbias", [128, 8, 2, 16], F32)
        tri = P.sb("tri", [128, 128], BF16)
        gm = P.sb("gm", [128, 2, 16], F32)
        top8 = P.sb("top8", [128, 2, 8], F32)
        thr = P.sb("thr", [128, 2], F32)
        sel = [P.sb(f"sel{i}", [128, 2, 16], F32) for i in range(2)]
        pTs = [P.sb(f"pTs{i}", [128, 512], BF16) for i in range(3)]
        acc = [P.sb(f"acc{i}", [128, 2, 132], F32) for i in range(2)]
        rec = P.sb("rec", [128, 2], F32)
        osb = [P.sb(f"osb{i}", [128, 2, 132], F32) for i in range(2)]
        ab = [P.sb(f"ab{i}", [128, 2, 128], BF16) for i in range(2)]
        pS = [P.ps(f"pS{i}", [128, 512], F32) for i in range(2)]
        pO = [P.ps(f"pO{i}", [128, 2, 256], F32) for i in range(2)]
        pG = P.ps("pG", [128, 2, 16], F32)
        pT = P.ps("pT", [128, 512], BF16)

        P.dma(gbias[:], io["gbias"], writes=[gbias], tb=gbias)
        P.dma(tri[:], io["tri"], writes=[tri], tb=tri)
        for i in range(2):
            P.do("pool", lambda e, i=i: e.memset(Vb[i][:, :, 128:129], 1.0), writes=[Vb[i]])

        def hload(h):
            b = h % 2
            P.dma(qT[b][:], io["qT"][h], writes=[qT[b]], tb=qT[b])
            P.dma(kT[b][:], io["kT"][h], writes=[kT[b]], tb=kT[b])
            P.dma(Vb[b][:, :, 0:128], io["V"][h], writes=[Vb[b]], tb=Vb[b])

        hload(0)
        un = 0
        for h in range(16):
            if h + 1 < 16:
                hload(h + 1)
            q_, k_, v_, a_ = qT[h % 2], kT[h % 2], Vb[h % 2], aT[h % 2]
            P.do("act", lambda e, h=h: e.activation(out=kmb[:], in_=kmean[:, h, :], func=AF.Copy, scale=1.0 / 256.0), reads=[kmean], writes=[kmb])
            for qb in range(8):
                ac = acc[qb % 2]
                sl = sel[qb % 2]
                q0 = qb * 256
                for a in range(2):
                    P.do("pe", lambda e, a=a, q_=q_, q0=q0: e.matmul(pG[:, a, :], q_[:, q0 + a * 128:q0 + (a + 1) * 128], kmb[:], start=True, stop=True),
                         reads=[q_, kmb], writes=[pG], sig=(a == 1), check=(a == 0))
                P.do("dve", lambda e: e.tensor_copy(out=gm[:], in_=pG[:]), reads=[pG], writes=[gm])
                P.do("dve", lambda e, qb=qb: e.tensor_tensor(out=gm[:], in0=gm[:], in1=gbias[:, qb, :, :], op=ALU.add), reads=[gbias], writes=[gm])
                for a in range(2):
                    P.do("dve", lambda e, a=a: e.max(out=top8[:, a, :], in_=gm[:, a, :]), reads=[gm], writes=[top8])
                P.do("dve", lambda e: e.tensor_scalar(out=thr[:], in0=top8[:, :, 2], scalar1=-1.0e29, scalar2=None, op0=ALU.max), reads=[top8], writes=[thr])
                for a in range(2):
                    P.do("dve", lambda e, a=a, sl=sl: e.tensor_scalar(out=sl[:, a, :], in0=gm[:, a, :], scalar1=thr[:, a:a + 1], scalar2=None, op0=ALU.is_ge),
                         reads=[gm, thr], writes=[sl])
                so = 8 + qb
                for s in [so] + list(range(0, so)):
                    ps_ = pS[un % 2]
                    po_ = pO[un % 2]
                    pt = pTs[un % 3]
                    k0 = s * 256
                    if s == so:
                        P.do("pe", lambda e, ps_=ps_, k_=k_, q_=q_, k0=k0, q0=q0: e.matmul(ps_[:, 0:256], k_[:, k0:k0 + 128], q_[:, q0:q0 + 256], start=True, stop=True),
                             reads=[k_, q_], writes=[ps_], sig=False)
                        P.do("pe", lambda e, ps_=ps_, k_=k_, q_=q_, k0=k0, q0=q0: e.matmul(ps_[:, 256:384], k_[:, k0 + 128:k0 + 256], q_[:, q0 + 128:q0 + 256], start=True, stop=True),
                             reads=[k_, q_], writes=[ps_], check=False)
                        P.do("act", lambda e, ps_=ps_, pt=pt: e.activation(out=pt[:, 0:384], in_=ps_[:, 0:384], func=AF.Exp, scale=SCALE), reads=[ps_], writes=[pt])
                        P.do("pool", lambda e, pt=pt: e.tensor_tensor(out=pt[:, 0:128], in0=pt[:, 0:128], in1=tri[:], op=ALU.mult), reads=[tri], writes=[pt])
                        P.do("pool", lambda e, pt=pt: e.tensor_tensor(out=pt[:, 256:384], in0=pt[:, 256:384], in1=tri[:], op=ALU.mult), reads=[tri], writes=[pt])
                        P.do("pe", lambda e, po_=po_, pt=pt, v_=v_, s=s: e.matmul(po_[:, 0, 0:129], pt[:, 0:128], v_[:, s * 2, 0:129], start=True, stop=True),
                             reads=[pt, v_], writes=[po_], sig=False)
                        P.do("pe", lambda e, po_=po_, pt=pt, v_=v_, s=s: e.matmul(po_[:, 1, 0:129], pt[:, 128:256], v_[:, s * 2, 0:129], start=True, stop=False),
                             reads=[pt, v_], writes=[po_], sig=False, check=False)
                        P.do("pe", lambda e, po_=po_, pt=pt, v_=v_, s=s: e.matmul(po_[:, 1, 0:129], pt[:, 256:384], v_[:, s * 2 + 1, 0:129], start=False, stop=True),
                             reads=[pt, v_], writes=[po_], check=False)
                        P.do("dve", lambda e, ac=ac, po_=po_: e.tensor_copy(out=ac[:, :, 0:129], in_=po_[:, :, 0:129]), reads=[po_], writes=[ac])
                    else:
                        for kt in range(2):
                            P.do("pe", lambda e, ps_=ps_, k_=k_, q_=q_, k0=k0, q0=q0, kt=kt: e.matmul(ps_[:, kt * 256:(kt + 1) * 256], k_[:, k0 + kt * 128:k0 + (kt + 1) * 128],
                                                                                                  q_[:, q0:q0 + 256], start=True, stop=True),
                                 reads=[k_, q_], writes=[ps_], sig=(kt == 1), check=(kt == 0))
                        P.do("act", lambda e, ps_=ps_, pt=pt: e.activation(out=pt[:], in_=ps_[:], func=AF.Exp, scale=SCALE), reads=[ps_], writes=[pt])
                        for a in range(2):
                            for kt in range(2):
                                P.do("pe", lambda e, po_=po_, pt=pt, v_=v_, s=s, a=a, kt=kt: e.matmul(po_[:, a, 0:129], pt[:, kt * 256 + a * 128:kt * 256 + (a + 1) * 128],
                                                                                              v_[:, s * 2 + kt, 0:129], start=(kt == 0), stop=(kt == 1)),
                                     reads=[pt, v_], writes=[po_], sig=(a == 1 and kt == 1), check=(a == 0 and kt == 0))
                        ob = osb[un % 2]
                        for a in range(2):
                            P.do("act", lambda e, ob=ob, po_=po_, sl=sl, a=a, s=s: e.activation(out=ob[:, a, 0:129], in_=po_[:, a, 0:129], func=AF.Copy, scale=sl[:, a, s:s + 1]),
                                 reads=[po_, sl], writes=[ob])
                        P.do("dve", lambda e, ac=ac, ob=ob: e.tensor_tensor(out=ac[:, :, 0:129], in0=ac[:, :, 0:129], in1=ob[:, :, 0:129], op=ALU.add),
                             reads=[ob], writes=[ac])
                    un += 1
                ab_ = ab[qb % 2]
                P.do("dve", lambda e, ac=ac: e.reciprocal(out=rec[:], in_=ac[:, :, 128]), reads=[ac], writes=[rec])
                for a in range(2):
                    P.do("dve", lambda e, ac=ac, ab_=ab_, a=a: e.tensor_scalar(out=ab_[:, a, :], in0=ac[:, a, 0:128], scalar1=rec[:, a:a + 1], scalar2=None, op0=ALU.mult),
                         reads=[ac, rec], writes=[ab_])
                for a in range(2):
                    c0 = (qb % 2) * 256 + a * 128
                    P.do("pe", lambda e, ab_=ab_, a=a, c0=c0: e.transpose(pT[:, c0:c0 + 128], ab_[:, a, :], ident[:]),
                         reads=[ab_, ident], writes=[pT], sig=(a == 1), check=(a == 0))
                if qb % 2 == 1:
                    P.do("act", lambda e, a_=a_, qb=qb: e.activation(out=a_[:, (qb - 1) * 256:(qb + 1) * 256], in_=pT[:], func=AF.Copy), reads=[pT], writes=[a_])
                nq = h * 8 + qb + 1
                while cst["i"] < min(len(cchunks), (nq * len(cchunks) + 127) // 128):
                    conv_step()
            P.dma(io["mixT"][h], a_[:], reads=[a_], tb=a_)
        while cst["i"] < len(cchunks):
            conv_step()
        P.finish()


def phase_de(nc, pool, G, io):
    TT = 512
    NT = TT // 128
    NG = HALF // TT
    with ExitStack() as st:
        P = Prog(nc, st, pool, "E")
        ident = G["ident"]
        hid = P.sb("hid", [128, 32, TT], BF16)
        y = P.sb("y", [128, NT, D], F32)
        hT = P.sb("hT", [128, 32, TT], BF16)
        xch = [P.sb(f"xch{i}", [128, 256], F32) for i in range(3)]
        gch = [P.sb(f"gch{i}", [128, 512], F32) for i in range(2)]
        bch = [P.sb(f"bch{i}", [128, 512], F32) for i in range(2)]
        rl = [P.sb(f"rl{i}", [128, TT], F32) for i in range(2)]
        evt = [P.sb(f"evt{i}", [128, 256], F32) for i in range(2)]
        hb = [P.sb(f"hb{i}", [128, 512], BF16) for i in range(2)]
        stats = P.sb("stats", [128, 8, 6], F32)
        mv = P.sb("mv", [128, 2], F32)
        sd = P.sb("sd", [128, 1], F32)
        rstd = P.sb("rstd", [128, 1], F32)
        nmr = P.sb("nmr", [128, 1], F32)
        epsb = P.sb("epsb", [128, 1], F32)
        pacc = [P.ps(f"pacc{i}", [128, 512], F32) for i in range(2)]
        pac2 = [P.ps(f"pac2{i}", [128, 512], F32) for i in range(4)]
        pT = P.ps("pT", [128, 512], BF16)
        P.do("pool", lambda e: e.memset(epsb[:], EPS), writes=[epsb])

        msrc = []
        w1src = []
        for tg in range(NG):
            msrc += [io["wout_t"][cb] for cb in range(16)]
            msrc += [io["wff2_t"][cb][:, kg * 32:(kg + 1) * 32, :] for kg in range(4) for cb in range(16)]
            w1src += [io["wff1_t"][ft] for ft in range(128)]
        wm = WStream(P, "wm", [128, 32, 256], 2, msrc)
        w1 = WStream(P, "w1", [128, 32, 128], 3, w1src)
        cnt = {"mi": 0, "fi": 0, "ev": 0, "ci": 0, "p2": 0, "xi": 0}

        def layer_norm(tt, gsrc, bsrc, tok0, final):
            for c in range(8):
                P.do("dve", lambda e, c=c: e.bn_stats(out=stats[:, c, :], in_=y[:, tt, c * 512:(c + 1) * 512]), reads=[y], writes=[stats])
            P.do("dve", lambda e: e.bn_aggr(out=mv[:], in_=stats[:].rearrange("p a b -> p (a b)")), reads=[stats], writes=[mv])
            P.do("act", lambda e: e.activation(out=sd[:], in_=mv[:, 1:2], func=AF.Sqrt, bias=epsb[:], scale=1.0), reads=[mv, epsb], writes=[sd])
            P.do("dve", lambda e: e.reciprocal(out=rstd[:], in_=sd[:]), reads=[sd], writes=[rstd])
            P.do("dve", lambda e: e.scalar_tensor_tensor(out=nmr[:], in0=mv[:, 0:1], scalar=-1.0, in1=rstd[:], op0=ALU.mult, op1=ALU.mult), reads=[mv, rstd], writes=[nmr])
            for c in range(8):
                g_, b_ = gch[cnt["ci"] % 2], bch[cnt["ci"] % 2]
                cnt["ci"] += 1
                P.dma(g_[:], gsrc[:, c * 512:(c + 1) * 512].partition_broadcast(128), writes=[g_], tb=g_)
                P.dma(b_[:], bsrc[:, c * 512:(c + 1) * 512].partition_broadcast(128), writes=[b_], tb=b_)
                ysl = y[:, tt, c * 512:(c + 1) * 512]
                P.do("dve", lambda e, ysl=ysl: e.tensor_scalar(out=ysl, in0=ysl, scalar1=rstd[:, 0:1], scalar2=nmr[:, 0:1], op0=ALU.mult, op1=ALU.add), reads=[rstd, nmr], writes=[y])
                P.do("pool", lambda e, ysl=ysl, g_=g_: e.tensor_tensor(out=ysl, in0=ysl, in1=g_[:], op=ALU.mult), reads=[g_], writes=[y])
                P.do("pool", lambda e, ysl=ysl, b_=b_: e.tensor_tensor(out=ysl, in0=ysl, in1=b_[:], op=ALU.add), reads=[b_], writes=[y])
                if not final:
                    hb_ = hb[c % 2]
                    P.do("act", lambda e, ysl=ysl, hb_=hb_: e.activation(out=hb_[:], in_=ysl, func=AF.Copy), reads=[y], writes=[hb_])
                    for j in range(4):
                        P.do("pe", lambda e, hb_=hb_, j=j: e.transpose(pT[:, j * 128:(j + 1) * 128], hb_[:, j * 128:(j + 1) * 128], ident[:]),
                             reads=[hb_, ident], writes=[pT], sig=(j == 3), check=(j == 0))
                    P.do("dve", lambda e, c=c: e.tensor_copy(out=hT[:, c * 4:(c + 1) * 4, tt * 128:(tt + 1) * 128], in_=pT[:].rearrange("p (a t) -> p a t", t=128)),
                         reads=[pT], writes=[hT])
            if final:
                P.dma(io["out"][tok0 + tt * 128:tok0 + (tt + 1) * 128, :], y[:, tt, :], reads=[y], tb=y)

        for tg in range(NG):
            tok0 = tg * TT
            P.dma(hid[:], io["mixT"][:, :, tok0:tok0 + TT].rearrange("k p t -> p k t"), writes=[hid], tb=hid)
            for cb in range(16):
                w = wm.get(cnt["mi"])
                for tt in range(NT):
                    xc_ = xch[cnt["xi"] % 3]
                    cnt["xi"] += 1
                    P.dma(xc_[:], io["xo"][tok0 + tt * 128:tok0 + (tt + 1) * 128, cb * 256:(cb + 1) * 256], writes=[xc_], tb=xc_)
                    pa = pac2[cnt["p2"] % 4]
                    cnt["p2"] += 1
                    mm_group(P, pa[:, 0:256], pa, [(hid[:, kc, tt * 128:(tt + 1) * 128], w[:, kc, :]) for kc in range(32)], [hid, w])
                    et_ = evt[cnt["ev"] % 2]
                    cnt["ev"] += 1
                    P.do("act", lambda e, et_=et_, pa=pa: e.activation(out=et_[:], in_=pa[:, 0:256], func=AF.Copy), reads=[pa], writes=[et_])
                    P.do("dve", lambda e, tt=tt, cb=cb, xc_=xc_, et_=et_: e.scalar_tensor_tensor(out=y[:, tt, cb * 256:(cb + 1) * 256], in0=xc_[:], scalar=ALPHA, in1=et_[:],
                                                                                            op0=ALU.mult, op1=ALU.add),
                         reads=[xc_, et_], writes=[y])
                wm.release(cnt["mi"])
                cnt["mi"] += 1
            for tt in range(NT):
                layer_norm(tt, io["ln1_g"], io["ln1_b"], tok0, False)
            for g in range(4):
                for f in range(32):
                    w = w1.get(cnt["fi"])
                    pa = pacc[cnt["fi"] % 2]
                    r_ = rl[cnt["fi"] % 2]
                    mm_group(P, pa[:], pa, [(w[:, kc, :], hT[:, kc, :]) for kc in range(32)], [w, hT])
                    w1.release(cnt["fi"])
                    cnt["fi"] += 1
                    P.do("act", lambda e, r_=r_, pa=pa: e.activation(out=r_[:], in_=pa[:], func=AF.Relu), reads=[pa], writes=[r_])
                    P.do("pool", lambda e, r_=r_, f=f: e.tensor_tensor(out=hid[:, f, :], in0=r_[:], in1=r_[:], op=ALU.mult), reads=[r_], writes=[hid])
                for cb in range(16):
                    w = wm.get(cnt["mi"])
                    for tt in range(NT):
                        pa = pac2[cnt["p2"] % 4]
                        cnt["p2"] += 1
                        mm_group(P, pa[:, 0:256], pa, [(hid[:, k, tt * 128:(tt + 1) * 128], w[:, k, :]) for k in range(32)], [hid, w])
                        et_ = evt[cnt["ev"] % 2]
                        cnt["ev"] += 1
                        ysl = y[:, tt, cb * 256:(cb + 1) * 256]
                        P.do("act", lambda e, et_=et_, pa=pa: e.activation(out=et_[:], in_=pa[:, 0:256], func=AF.Copy), reads=[pa], writes=[et_])
                        P.do("dve", lambda e, ysl=ysl, et_=et_, g=g: e.scalar_tensor_tensor(out=ysl, in0=ysl, scalar=(ALPHA if g == 0 else 1.0), in1=et_[:], op0=ALU.mult, op1=ALU.add),
                             reads=[et_], writes=[y])
                    wm.release(cnt["mi"])
                    cnt["mi"] += 1
            for tt in range(NT):
                layer_norm(tt, io["ln2_g"], io["ln2_b"], tok0, True)
        P.finish()


def build(phases=None, dbg=False):
    nc = bass.Bass("TRN2", target_bir_lowering=False)

    def din(name, shape, dt=F32):
        return nc.dram_tensor(name, shape, dt, kind="ExternalInput").ap()

    def scr(name, shape, dt=BF16):
        kind = "ExternalOutput" if (dbg and name in ("qT", "kT", "V", "mixT")) else "Internal"
        return nc.dram_tensor(name, shape, dt, kind=kind).ap()

    io = {}
    io["xo"] = din("xo", [HALF, D])
    io["xc"] = din("xc", [HALF, D])
    w_in = din("w_in", [D, 8192])
    w_pool = din("w_pool", [2048, 512])
    w_out = din("w_out", [D, D])
    w_ff1 = din("w_ff1", [D, DFF])
    w_ff2 = din("w_ff2", [DFF, D])
    for n in ("ln1_g", "ln1_b", "ln2_g", "ln2_b"):
        io[n] = din(n, [1, D])
    io["cosT"] = din("cosT", [128, S])
    io["sinT"] = din("sinT", [128, S])
    io["psc"] = din("psc", [128, 16])
    io["invc"] = din("invc", [128, 4, 16])
    io["gbias"] = din("gbias", [128, 8, 2, 16])
    c_ident = din("ident", [128, 128], BF16)
    c_rm = din("rm", [128, 128], BF16)
    io["tri"] = din("tri", [128, 128], BF16)
    io["out"] = nc.dram_tensor("out", [HALF, D], F32, kind="ExternalOutput").ap()
    io["win_t"] = scr("win_t", [64, 128, 32, 128])
    io["wff1_t"] = scr("wff1_t", [128, 128, 32, 128])
    io["wout_t"] = scr("wout_t", [16, 128, 32, 256])
    io["wff2_t"] = scr("wff2_t", [16, 128, 128, 256])
    io["wpool_t"] = scr("wpool_t", [1, 128, 16, 512])
    io["qT"] = scr("qT", [16, 128, HALF])
    io["kT"] = scr("kT", [16, 128, S])
    io["V"] = scr("V", [16, 128, 32, 128])
    io["mixT"] = scr("mixT", [32, 128, HALF])

    with ExitStack() as gst:
        pool = [Sem(gst.enter_context(nc.semaphore(f"gs{i}")), f"gs{i}") for i in range(96)]
        G = {}
        G["ident"] = TB(gst.enter_context(nc.sbuf_tensor("g_ident", [128, 128], BF16)), const=True)
        G["rm"] = TB(gst.enter_context(nc.sbuf_tensor("g_rm", [128, 128], BF16)), const=True)
        G["kmean"] = TB(gst.enter_context(nc.sbuf_tensor("g_kmean", [128, 16, 16], F32)))
        with ExitStack() as st:
            P = Prog(nc, st, pool, "I")
            P.dma(G["ident"][:], c_ident, tb=G["ident"])
            P.dma(G["rm"][:], c_rm, tb=G["rm"])
            P.finish()
        ph = phases or ("W0", "W1", "A", "W2", "W3", "W4", "B", "E")
        if "W0" in ph:
            phase_convert(nc, pool, w_in, io["win_t"], D, 8192, 128, "W0")
        if "W1" in ph:
            phase_convert(nc, pool, w_pool, io["wpool_t"], 2048, 512, 512, "W1")
        if "A" in ph:
            phase_a(nc, pool, G, io)
        if "B" in ph:
            specs = [(w_out, io["wout_t"], D, D, 256), (w_ff1, io["wff1_t"], D, DFF, 128), (w_ff2, io["wff2_t"], DFF, D, 256)]
            phase_b(nc, pool, G, io, specs if "W2" in ph else ())
        if "E" in ph:
            phase_de(nc, pool, G, io)
    return nc


_NC = None
_DEBUG_RET_MAPS = False


def _consts():
    bf = ml_dtypes.bfloat16
    ident = np.eye(128, dtype=np.float32).astype(bf)
    rm = np.zeros((128, 128), np.float32)
    for m in range(64):
        rm[m + 64, m] = -1.0
        rm[m, m + 64] = 1.0
    rm = rm.astype(bf)
    kk = np.arange(128)[:, None]
    qq = np.arange(128)[None, :]
    tri = (kk <= qq).astype(np.float32).astype(bf)
    inv_freq = (1.0 / (np.float32(10000.0) ** (np.arange(0, 128, 2, dtype=np.float32) / np.float32(128)))).astype(np.float32)
    return ident, rm, tri, inv_freq


def kernel(x, w_in, w_pool, pool_scale, w_out, ln1_g, ln1_b, w_ff1, w_ff2, ln2_g, ln2_b):
    global _NC
    if _NC is None:
        _NC = build()
    nc = _NC
    ident, rm, tri, inv_freq = _consts()
    x = np.asarray(x, np.float32)
    f = lambda a: np.ascontiguousarray(np.asarray(a, np.float32))
    shared = {
        "w_in": f(w_in[0]), "w_pool": f(np.asarray(w_pool[0]).reshape(2048, 512)), "w_out": f(w_out[0]),
        "w_ff1": f(w_ff1[0]), "w_ff2": f(w_ff2[0]),
        "ln1_g": f(ln1_g), "ln1_b": f(ln1_b), "ln2_g": f(ln2_g), "ln2_b": f(ln2_b),
        "psc": f(np.asarray(pool_scale[0]).reshape(16, 128).T),
        "ident": ident, "rm": rm, "tri": tri,
    }
    in_maps = []
    for c in range(8):
        b, half = c // 2, c % 2
        xo = f(x[b, half * HALF:(half + 1) * HALF])
        xc = f(x[b, 0:HALF]) if half == 1 else np.zeros((HALF, D), np.float32)
        pos = np.concatenate([np.arange(HALF), half * HALF + np.arange(HALF)]).astype(np.float32)
        ang = pos[:, None] * inv_freq[None, :]
        cosv = np.cos(ang).astype(np.float32)
        sinv = np.sin(ang).astype(np.float32)
        cosT = np.ascontiguousarray(np.concatenate([cosv, cosv], axis=1).T)
        sinT = np.ascontiguousarray(np.concatenate([sinv, sinv], axis=1).T)
        gb = np.full((8, 16), NEG, np.float32)
        for i in range(8):
            for s in range(16):
                if s < 8 + i and (half == 1 or s >= 8):
                    gb[i, s] = 0.0
        gbias = np.ascontiguousarray(np.broadcast_to(gb[None, :, None, :], (128, 8, 2, 16))).astype(np.float32)
        invc = np.zeros((4, 16), np.float32)
        for g, w in enumerate((2, 4, 8, 16)):
            for t in range(16):
                invc[g, t] = 1.0 / (min(t + 1, w) if half == 0 else w)
        invc = np.ascontiguousarray(np.broadcast_to(invc[None], (128, 4, 16))).astype(np.float32)
        m = dict(shared)
        m.update({"xo": xo, "xc": xc, "cosT": cosT, "sinT": sinT, "gbias": gbias, "invc": invc})
        in_maps.append(m)
    if _DEBUG_RET_MAPS:
        return in_maps
    res = run_bass_kernel_spmd(nc, in_maps, core_ids=list(range(8)))
    out = np.empty((NB_, S, D), np.float32)
    for c in range(8):
        b, half = c // 2, c % 2
        out[b, half * HALF:(half + 1) * HALF] = np.asarray(res.results[c]["out"], np.float32)
    return out
```
